# Optimizing a Trainium2 kernel written in Bass

```python
import jax, jax.numpy as jnp
from jax import lax
import numpy as np

D_MODEL = 1024
BATCH = 8
SEQ = 4096
DEPTH = 1

GRID_W = 64
CTX_LEN = 256
HEAD_DIM = 64
A_HEADS = 8
A_KV_HEADS = 2
A_WINDOW = 128
A_BLOCK = 128
B_HEADS = 8
NA_MAX_KH = 8
NA_KW = 16
NA_ROW_BLOCK = 2
A_WIDTH = A_HEADS * HEAD_DIM
A_KV_WIDTH = A_KV_HEADS * HEAD_DIM
B_WIDTH = B_HEADS * HEAD_DIM
MIX_WIDTH = A_WIDTH + B_WIDTH
IN_COLS = A_WIDTH + 2 * A_KV_WIDTH + 3 * B_WIDTH
FFN_HIDDEN = -(-8 * D_MODEL // (3 * 256)) * 256
N_MOD = 6
ROPE_BASE = 10000.0
EPS = 1e-6

kernel_name = "hybrid_window_neighbourhood_dit_block"


def _rms_norm(x, g):
    xf = x.astype(jnp.float32)
    y = xf * lax.rsqrt(jnp.mean(xf * xf, axis=-1, keepdims=True) + EPS)
    return (y * g.astype(jnp.float32)).astype(x.dtype)


def _modulate(h, shift, scale):
    return h * (1 + scale) + shift


def _axial_rope(x, rows, cols):
    half = HEAD_DIM // 2
    quarter = half // 2
    inv_freq = 1.0 / (ROPE_BASE ** (jnp.arange(quarter, dtype=jnp.float32) / quarter))

    def rot(xh, pos):
        ang = pos.astype(jnp.float32)[:, None] * inv_freq[None, :]
        cos = jnp.cos(ang)[None, :, None, :]
        sin = jnp.sin(ang)[None, :, None, :]
        x1 = xh[..., :quarter].astype(jnp.float32)
        x2 = xh[..., quarter:].astype(jnp.float32)
        return jnp.concatenate([x1 * cos - x2 * sin, x1 * sin + x2 * cos], axis=-1)

    out = jnp.concatenate([rot(x[..., :half], rows), rot(x[..., half:], cols)], axis=-1)
    return out.astype(x.dtype)


def _in_proj(h, w_in):
    B, L, _ = h.shape
    p = h @ w_in
    offs = np.cumsum([A_WIDTH, A_KV_WIDTH, A_KV_WIDTH, B_WIDTH, B_WIDTH])
    q_a, k_a, v_a, q_b, k_b, v_b = jnp.split(p, offs, axis=-1)
    hd = lambda t, n: t.reshape(B, L, n, HEAD_DIM)
    return (hd(q_a, A_HEADS), hd(k_a, A_KV_HEADS), hd(v_a, A_KV_HEADS),
            hd(q_b, B_HEADS), hd(k_b, B_HEADS), hd(v_b, B_HEADS))


def _band_mask(S):
    nb = S // A_BLOCK
    span = A_BLOCK + 2 * A_WINDOW
    qpos = np.arange(nb)[:, None, None] * A_BLOCK + np.arange(A_BLOCK)[None, :, None]
    kpos = np.arange(nb)[:, None, None] * A_BLOCK - A_WINDOW + np.arange(span)[None, None, :]
    return (np.abs(kpos - qpos) <= A_WINDOW) & (kpos >= 0) & (kpos < S)


def _window_gqa_latent(q, k, v, k_ctx, v_ctx, sink):
    B, S = q.shape[:2]
    nb = S // A_BLOCK
    G = A_HEADS // A_KV_HEADS
    n_side = A_WINDOW // A_BLOCK
    scale = HEAD_DIM ** -0.5
    pad = ((0, 0), (A_WINDOW, A_WINDOW), (0, 0), (0, 0))
    kp = jnp.pad(k, pad).reshape(B, nb + 2 * n_side, A_BLOCK, A_KV_HEADS, HEAD_DIM)
    vp = jnp.pad(v, pad).reshape(B, nb + 2 * n_side, A_BLOCK, A_KV_HEADS, HEAD_DIM)
    kb = jnp.concatenate([kp[:, o:o + nb] for o in range(2 * n_side + 1)], axis=2)
    vb = jnp.concatenate([vp[:, o:o + nb] for o in range(2 * n_side + 1)], axis=2)
    span = kb.shape[2]
    qb = q.reshape(B, nb, A_BLOCK, A_KV_HEADS, G, HEAD_DIM)
    mask = jnp.asarray(_band_mask(S))[:, None, None]
    s_loc = jnp.einsum('bnqkgd,bnjkd->bnkgqj', qb, kb).astype(jnp.float32) * scale
    s_loc = jnp.where(mask, s_loc, -jnp.inf)
    s_ctx = jnp.einsum('bnqkgd,bjkd->bnkgqj', qb, k_ctx).astype(jnp.float32) * scale
    sink_col = jnp.broadcast_to(sink.astype(jnp.float32).reshape(1, 1, A_KV_HEADS, G, 1, 1),
                                s_loc.shape[:-1] + (1,))
    p = jax.nn.softmax(jnp.concatenate([s_loc, s_ctx, sink_col], axis=-1), axis=-1)
    L = k_ctx.shape[1]
    p_loc = p[..., :span].astype(v.dtype)
    p_ctx = p[..., span:span + L].astype(v.dtype)
    o = (jnp.einsum('bnkgqj,bnjkd->bnqkgd', p_loc, vb)
         + jnp.einsum('bnkgqj,bjkd->bnqkgd', p_ctx, v_ctx))
    return o.reshape(B, S, A_WIDTH)


def _na_pattern(rows):
    kh = min(NA_MAX_KH, rows)
    n_blk = rows // NA_ROW_BLOCK
    n_kr = min(NA_ROW_BLOCK + kh - 1, rows)
    r0 = np.arange(n_blk) * NA_ROW_BLOCK
    row_start = lambda r: np.clip(r - kh // 2, 0, rows - kh)
    k_start = np.minimum(row_start(r0), rows - n_kr)
    key_rows = k_start[:, None] + np.arange(n_kr)[None, :]
    qi = np.arange(NA_ROW_BLOCK * GRID_W)
    q_r = r0[:, None] + qi[None, :] // GRID_W
    q_c = qi % GRID_W
    kj = np.arange(n_kr * GRID_W)
    k_r = key_rows[:, kj // GRID_W]
    k_c = kj % GRID_W
    rs = row_start(q_r)
    cs = np.clip(q_c - NA_KW // 2, 0, GRID_W - NA_KW)
    row_ok = (k_r[:, None, :] >= rs[:, :, None]) & (k_r[:, None, :] < rs[:, :, None] + kh)
    col_ok = (k_c[None, :] >= cs[:, None]) & (k_c[None, :] < cs[:, None] + NA_KW)
    mask = row_ok & col_ok[None]
    dr = np.clip(k_r[:, None, :] - q_r[:, :, None] + NA_MAX_KH - 1, 0, 2 * NA_MAX_KH - 2)
    dc = np.broadcast_to(np.clip(k_c[None, :] - q_c[:, None] + NA_KW - 1, 0, 2 * NA_KW - 2)[None],
                         mask.shape)
    return key_rows, mask, dr, dc


def _neighbourhood_latent(q, k, v, k_ctx, v_ctx, rpb):
    B, S = q.shape[:2]
    rows = S // GRID_W
    key_rows, mask, dr, dc = _na_pattern(rows)
    n_blk, n_kr = key_rows.shape
    scale = HEAD_DIM ** -0.5
    idx = jnp.asarray(key_rows)
    kg = k.reshape(B, rows, GRID_W, B_HEADS, HEAD_DIM)[:, idx].reshape(
        B, n_blk, n_kr * GRID_W, B_HEADS, HEAD_DIM)
    vg = v.reshape(B, rows, GRID_W, B_HEADS, HEAD_DIM)[:, idx].reshape(
        B, n_blk, n_kr * GRID_W, B_HEADS, HEAD_DIM)
    qb = q.reshape(B, n_blk, NA_ROW_BLOCK * GRID_W, B_HEADS, HEAD_DIM)
    bias = jnp.transpose(rpb[:, jnp.asarray(dr), jnp.asarray(dc)], (1, 0, 2, 3)).astype(jnp.float32)
    s_loc = jnp.einsum('bnqhd,bnjhd->bnhqj', qb, kg).astype(jnp.float32) * scale + bias
    s_loc = jnp.where(jnp.asarray(mask)[:, None], s_loc, -jnp.inf)
    s_ctx = jnp.einsum('bnqhd,bjhd->bnhqj', qb, k_ctx).astype(jnp.float32) * scale
    p = jax.nn.softmax(jnp.concatenate([s_loc, s_ctx], axis=-1), axis=-1)
    kb_len = kg.shape[2]
    p_loc = p[..., :kb_len].astype(v.dtype)
    p_ctx = p[..., kb_len:].astype(v.dtype)
    o = (jnp.einsum('bnhqj,bnjhd->bnqhd', p_loc, vg)
         + jnp.einsum('bnhqj,bjhd->bnqhd', p_ctx, v_ctx))
    return o.reshape(B, S, B_WIDTH)


def _ctx_self_attention(q, k, v, n_kv, sink=None):
    B, L, H, _ = q.shape
    G = H // n_kv
    qg = q.reshape(B, L, n_kv, G, HEAD_DIM)
    s = jnp.einsum('blkgd,bjkd->bkglj', qg, k).astype(jnp.float32) * HEAD_DIM ** -0.5
    if sink is not None:
        sink_col = jnp.broadcast_to(sink.astype(jnp.float32).reshape(1, n_kv, G, 1, 1), s.shape[:-1] + (1,))
        s = jnp.concatenate([s, sink_col], axis=-1)
    p = jax.nn.softmax(s, axis=-1)[..., :L].astype(v.dtype)
    o = jnp.einsum('bkglj,bjkd->blkgd', p, v)
    return o.reshape(B, L, H * HEAD_DIM)


def _merge(o_a, o_b, g_a, g_b, w_out):
    return jnp.concatenate([_rms_norm(o_a, g_a), _rms_norm(o_b, g_b)], axis=-1) @ w_out


def _swiglu(h, w_gate, w_up, w_down):
    return (jax.nn.silu(h @ w_gate) * (h @ w_up)) @ w_down


def setup_inputs(seed: int = 0) -> dict:
    key = jax.random.key(seed)
    ks = jax.random.split(key, 24)
    nrm = lambda k, shape: jax.random.normal(k, shape, dtype=jnp.float32)
    gain = lambda k, shape: 1.0 + 0.05 * nrm(k, shape)
    D = D_MODEL
    return {
        "x": nrm(ks[0], (BATCH, SEQ, D)),
        "c": nrm(ks[1], (BATCH, D)),
        "ctx": nrm(ks[2], (BATCH, CTX_LEN, D)),
        "c_ctx": nrm(ks[3], (D,)),
        "w_mod": nrm(ks[4], (DEPTH, D, N_MOD * D)) * (0.5 * D ** -0.5),
        "b_mod": 0.01 * nrm(ks[5], (DEPTH, N_MOD * D)),
        "norm1_g": gain(ks[6], (DEPTH, D)),
        "w_in": nrm(ks[7], (DEPTH, D, IN_COLS)) * D ** -0.5,
        "qn_a": gain(ks[8], (DEPTH, HEAD_DIM)),
        "kn_a": gain(ks[9], (DEPTH, HEAD_DIM)),
        "sink_a": 0.5 * nrm(ks[10], (DEPTH, A_HEADS)),
        "qn_b": gain(ks[11], (DEPTH, HEAD_DIM)),
        "kn_b": gain(ks[12], (DEPTH, HEAD_DIM)),
        "rpb_b": 0.1 * nrm(ks[13], (DEPTH, B_HEADS, 2 * NA_MAX_KH - 1, 2 * NA_KW - 1)),
        "on_a": gain(ks[14], (DEPTH, A_WIDTH)),
        "on_b": gain(ks[15], (DEPTH, B_WIDTH)),
        "w_out": nrm(ks[16], (DEPTH, MIX_WIDTH, D)) * MIX_WIDTH ** -0.5,
        "norm2_g": gain(ks[17], (DEPTH, D)),
        "w_gate": nrm(ks[18], (DEPTH, D, FFN_HIDDEN)) * D ** -0.5,
        "w_up": nrm(ks[19], (DEPTH, D, FFN_HIDDEN)) * D ** -0.5,
        "w_down": nrm(ks[20], (DEPTH, FFN_HIDDEN, D)) * FFN_HIDDEN ** -0.5,
    }


def reference(x, c, ctx, c_ctx, w_mod, b_mod, norm1_g, w_in, qn_a, kn_a, sink_a, qn_b, kn_b,
              rpb_b, on_a, on_b, w_out, norm2_g, w_gate, w_up, w_down):
    S = x.shape[1]
    t = jnp.arange(S)
    row_pos = t // GRID_W
    col_pos = t % GRID_W
    for l in range(DEPTH):
        mod = (jax.nn.silu(c) @ w_mod[l] + b_mod[l])[:, None, :]
        mod_c = (jax.nn.silu(c_ctx) @ w_mod[l] + b_mod[l])[None, None, :]
        sh1, sc1, g1, sh2, sc2, g2 = jnp.split(mod, N_MOD, axis=-1)
        csh1, csc1, cg1, csh2, csc2, cg2 = jnp.split(mod_c, N_MOD, axis=-1)

        h = _modulate(_rms_norm(x, norm1_g[l]), sh1, sc1)
        hc = _modulate(_rms_norm(ctx, norm1_g[l]), csh1, csc1)
        q_a, k_a, v_a, q_b, k_b, v_b = _in_proj(h, w_in[l])
        q_ac, k_ac, v_ac, q_bc, k_bc, v_bc = _in_proj(hc, w_in[l])

        q_a = _axial_rope(_rms_norm(q_a, qn_a[l]), row_pos, col_pos)
        k_a = _axial_rope(_rms_norm(k_a, kn_a[l]), row_pos, col_pos)
        q_ac, k_ac = _rms_norm(q_ac, qn_a[l]), _rms_norm(k_ac, kn_a[l])
        q_b, k_b = _rms_norm(q_b, qn_b[l]), _rms_norm(k_b, kn_b[l])
        q_bc, k_bc = _rms_norm(q_bc, qn_b[l]), _rms_norm(k_bc, kn_b[l])

        o_a = _window_gqa_latent(q_a, k_a, v_a, k_ac, v_ac, sink_a[l])
        o_b = _neighbourhood_latent(q_b, k_b, v_b, k_bc, v_bc, rpb_b[l])
        x_new = x + g1 * _merge(o_a, o_b, on_a[l], on_b[l], w_out[l])
        h2 = _modulate(_rms_norm(x_new, norm2_g[l]), sh2, sc2)
        x_new = x_new + g2 * _swiglu(h2, w_gate[l], w_up[l], w_down[l])

        if l < DEPTH - 1:
            o_ac = _ctx_self_attention(q_ac, k_ac, v_ac, A_KV_HEADS, sink_a[l])
            o_bc = _ctx_self_attention(q_bc, k_bc, v_bc, B_HEADS)
            ctx = ctx + cg1 * _merge(o_ac, o_bc, on_a[l], on_b[l], w_out[l])
            hc2 = _modulate(_rms_norm(ctx, norm2_g[l]), csh2, csc2)
            ctx = ctx + cg2 * _swiglu(hc2, w_gate[l], w_up[l], w_down[l])
        x = x_new
    return x
```

```python
import contextlib
import os
import numpy as np
import concourse.bass as bass
import concourse.mybir as mybir
from concourse.bass_utils import run_bass_kernel_spmd

F32 = mybir.dt.float32
BF16 = mybir.dt.bfloat16
AF = mybir.ActivationFunctionType
ALU = mybir.AluOpType
AX = mybir.AxisListType

S, D, CTX, HD = 4096, 1024, 256, 64
NT = S // 128
INC = 2304
FH = 2816
NJ = FH // 128
EPS = 1e-6
NEG = -30000.0
RK = 7
RKV = 8
RQ = 4
SAME_ENGINE_SYNC = True
FULL_SAME_ENGINE = True


class Buf:
    __slots__ = ("name", "writers", "readers")

    def __init__(self, name):
        self.name = name
        self.writers = []
        self.readers = []


class Op:
    __slots__ = ("eng", "fn", "deps", "raw", "dma_key", "idx", "signal", "tick", "_need", "cost", "pos")

    def __init__(self, eng, fn, dma_key, idx, cost=None):
        self.cost = cost
        self.pos = idx
        self.eng = eng
        self.fn = fn
        self.deps = set()
        self.raw = set()
        self.dma_key = dma_key
        self.idx = idx
        self.signal = False
        self.tick = None


class Prog:
    ENGS = ("pe", "act", "dve", "pool", "sp")

    def __init__(self, nc, tag):
        self.nc = nc
        self.ops = []
        self.tag = tag

    DEF_COST = {"pe": 0.08, "act": 0.55, "dve": 0.5, "pool": 1.0, "sp": 0.05}

    def add(self, eng, fn, reads=(), writes=(), dma_key=None, cost=None):
        if cost is None:
            cost = 4.0 if dma_key is not None else self.DEF_COST[eng]
        op = Op(eng, fn, dma_key, len(self.ops), cost)
        for b in reads:
            op.deps.update(b.writers)
            op.raw.update(b.writers)
        for b in writes:
            op.deps.update(b.readers)
            op.deps.update(b.writers)
        for b in reads:
            b.readers.append(op.idx)
        for b in writes:
            if b.readers:
                b.writers = [op.idx]
                b.readers = []
            else:
                b.writers.append(op.idx)
        op.deps.discard(op.idx)
        self.ops.append(op)
        return op

    def _needs_wait(self, op, d):
        if d.dma_key is not None:
            return True
        if d.eng != op.eng:
            return True
        if op.eng in ("pe", "sp"):
            return False
        return SAME_ENGINE_SYNC and op.dma_key is None and (d.idx in op.raw or FULL_SAME_ENGINE)

    def schedule(self, window):
        ops = self.ops
        LAT = 0.3
        finish = [None] * len(ops)
        tail = [0.0] * len(ops)
        for op in reversed(ops):
            tl = tail[op.idx] + op.cost
            tail[op.idx] = tl
            for di in op.deps:
                if tail[di] < tl:
                    tail[di] = tl
        use_tail = os.environ.get("SCHED_TAIL", "1") == "1"
        remaining = {e: [o.idx for o in ops if o.eng == e] for e in self.ENGS}
        t_eng = {e: 0.0 for e in self.ENGS}
        order = {e: [] for e in self.ENGS}
        left = len(ops)
        while left:
            best = None
            for e in self.ENGS:
                lst = remaining[e]
                te = t_eng[e]
                for pos in range(min(window, len(lst))):
                    op = ops[lst[pos]]
                    ready = te
                    ok = True
                    for di in op.deps:
                        f = finish[di]
                        if f is None:
                            ok = False
                            break
                        d = ops[di]
                        if d.eng != e or d.dma_key is not None:
                            f += LAT
                        if f > ready:
                            ready = f
                    if not ok:
                        continue
                    key = (ready, -tail[op.idx], op.idx) if use_tail else (ready, op.idx)
                    if best is None or key < best[0]:
                        best = (key, e, pos, ready)
                    if ready <= te and not use_tail:
                        break
            oi = best[0][-1]
            ready = best[0][0]
            e, pos = best[1], best[2]
            op = ops[oi]
            if op.dma_key is not None:
                t_eng[e] = ready + (1.0 if e == "pool" else 0.06)
                finish[oi] = ready + op.cost
            else:
                t_eng[e] = ready + op.cost
                finish[oi] = t_eng[e]
            remaining[e].pop(pos)
            op.pos = len(order[e])
            order[e].append(oi)
            left -= 1
        self.est_time = max(f for f in finish if f is not None)
        return order

    def emit(self, window=None):
        nc = self.nc
        ops = self.ops
        if window:
            order = self.schedule(window)
        else:
            order = {e: [o.idx for o in ops if o.eng == e] for e in self.ENGS}
            for e in self.ENGS:
                for p_, oi in enumerate(order[e]):
                    ops[oi].pos = p_
        for op in ops:
            if op.dma_key is not None:
                op.signal = True
            best = {}
            for di in op.deps:
                d = ops[di]
                if d.dma_key is None and self._needs_wait(op, d):
                    if d.eng not in best or ops[best[d.eng]].pos < d.pos:
                        best[d.eng] = di
            op._need = best
            for di in best.values():
                ops[di].signal = True
        last_dma = {}
        for e in self.ENGS:
            for oi in order[e]:
                if ops[oi].dma_key is not None:
                    last_dma[ops[oi].dma_key] = oi
        fin = Op("sp", lambda e: None, None, len(ops), 0.0)
        fin.deps = set(last_dma.values())
        fin._need = {}
        ops.append(fin)
        order["sp"].append(fin.idx)
        cnt = {}
        per_eng = {e: [ops[oi] for oi in order[e]] for e in self.ENGS}
        for e in self.ENGS:
            for op in per_eng[e]:
                if not op.signal:
                    continue
                key = ("dma", op.dma_key) if op.dma_key is not None else ("eng", op.eng)
                cnt[key] = cnt.get(key, 0) + (16 if op.dma_key is not None else 1)
                op.tick = (key, cnt[key])
        with contextlib.ExitStack() as st:
            sems = {}
            for n, k in enumerate(sorted(cnt.keys())):
                sems[k] = st.enter_context(nc.semaphore("s%s_%d" % (self.tag, n)))
            block = st.enter_context(nc.Block())

            def run(eng_name):
                def body(e):
                    seen = {}
                    for op in per_eng[eng_name]:
                        waits = {}
                        for di in op.deps:
                            d = ops[di]
                            if not self._needs_wait(op, d):
                                continue
                            if d.dma_key is None and op._need.get(d.eng) != di:
                                continue
                            k, v = d.tick
                            if seen.get(k, 0) >= v:
                                continue
                            if waits.get(k, 0) < v:
                                waits[k] = v
                        for k, v in waits.items():
                            e.wait_ge(sems[k], v)
                            seen[k] = v
                        inst = op.fn(e)
                        if op.signal:
                            inst.then_inc(sems[op.tick[0]], 16 if op.dma_key is not None else 1)
                return body

            if per_eng["pe"]:
                block.tensor(run("pe"))
            if per_eng["act"]:
                block.scalar(run("act"))
            if per_eng["dve"]:
                block.vector(run("dve"))
            if per_eng["pool"]:
                block.gpsimd(run("pool"))
            block.sync(run("sp"))
        return cnt


def _na_vis(n, t):
    i = np.arange(128)
    q_r = 2 * n + i // 64
    q_c = i % 64
    j = np.arange(128)
    k_r = 2 * t + j // 64
    k_c = j % 64
    rs = np.clip(q_r - 4, 0, 56)
    cs = np.clip(q_c - 8, 0, 48)
    row_ok = (k_r[:, None] >= rs[None, :]) & (k_r[:, None] < rs[None, :] + 8)
    col_ok = (k_c[:, None] >= cs[None, :]) & (k_c[:, None] < cs[None, :] + 16)
    dr = np.clip(k_r[:, None] - q_r[None, :] + 7, 0, 14)
    dc = np.clip(k_c[:, None] - q_c[None, :] + 15, 0, 30)
    return row_ok & col_ok, dr, dc


def _b_tiles(n):
    return [t for t in range(NT) if _na_vis(n, t)[0].any()]


EDGE_BLOCKS = (0, 1, 30, 31)


def _bias_id(n, t):
    if n in EDGE_BLOCKS:
        return 5 + 4 * EDGE_BLOCKS.index(n) + (t - _b_tiles(n)[0])
    return t - (n - 2)


def _host_consts(rpb):
    bias = np.full((21, 128, 8, 128), NEG, np.float32)
    done = set()
    for n in (2,) + EDGE_BLOCKS:
        for t in _b_tiles(n):
            vis, dr, dc = _na_vis(n, t)
            g = rpb[:, dr, dc]
            g = np.where(vis[None], g, np.float32(NEG))
            bias[_bias_id(n, t)] = np.transpose(g, (1, 0, 2))
            done.add(_bias_id(n, t))
    assert len(done) == 21
    bias = bias.reshape(21, 128, 1024)
    quarter = 16
    inv_freq = (1.0 / (10000.0 ** (np.arange(quarter, dtype=np.float32) / quarter))).astype(np.float32)
    t = np.arange(S)
    rows = (t // 64).astype(np.float32)
    cols = (t % 64).astype(np.float32)
    ar = rows[:, None] * inv_freq[None, :]
    ac = cols[:, None] * inv_freq[None, :]
    C = np.concatenate([np.cos(ar), np.cos(ar), np.cos(ac), np.cos(ac)], -1).astype(np.float32)
    Sn = np.concatenate([-np.sin(ar), np.sin(ar), -np.sin(ac), np.sin(ac)], -1).astype(np.float32)
    rope = np.stack([C.reshape(NT, 128, 64), Sn.reshape(NT, 128, 64)], 2)
    j = np.arange(128)[:, None]
    i = np.arange(128)[None, :]
    lo = np.where(j >= i, 0.0, NEG).astype(np.float32)
    hi = np.where(j <= i, 0.0, NEG).astype(np.float32)
    maskA = np.stack([np.tile(lo, (1, 4)), np.tile(hi, (1, 4))], 0)
    return bias, np.ascontiguousarray(rope), maskA


def build(debug=False, phases=('s0', 'a', 'b'), nt_a=None):
    nc = bass.Bass("TRN2", target_bir_lowering=False)
    din = lambda n, s: nc.dram_tensor(n, list(s), F32, kind="ExternalInput").ap()
    x_d = din("x", [S, D])
    ctx_d = din("ctx", [CTX, D])
    cT_d = din("cT", [128, 8, 2])
    wmod_d = din("w_mod", [D, 6 * D])
    bmod_d = din("b_mod", [1, 6 * D])
    n1g_d = din("n1g", [1, D])
    n2g_d = din("n2g", [1, D])
    win_d = din("w_in", [D, INC])
    gains_d = din("gains", [1, 4 * HD])
    sink_d = din("sink", [1, 8])
    on_d = din("on", [1, D])
    wout_d = din("w_out", [D, D])
    wg_d = din("w_gate", [D, FH])
    wu_d = din("w_up", [D, FH])
    wd_d = din("w_down", [FH, D])
    ident_d = din("ident", [128, 128])
    rope_d = din("rope", [NT, 128, 2, 64])
    maskA_d = din("maskA", [2, 128, 512])
    biasB_d = din("biasB", [21, 128, 1024])
    out_d = nc.dram_tensor("out", [S, D], F32, kind="ExternalOutput").ap()
    xs_d = nc.dram_tensor("xs_scratch", [S, D], F32, kind="Internal").ap()
    dbg_outs = {}

    outer = contextlib.ExitStack()
    with outer:
        def mk(stack):
            sb = lambda name, shape, dt=F32: stack.enter_context(nc.sbuf_tensor("t_" + name, list(shape), dt))
            ps = lambda name, shape, dt=F32: stack.enter_context(nc.psum_tensor("p_" + name, list(shape), dt))
            return sb, ps
        osb, _ = mk(outer)
        idf = osb("idf", [128, 128])
        idb = osb("idb", [128, 128], BF16)
        r2all = osb("r2all", [128, NT])
        gw2_bc = osb("gw2_bc", [128, D])
        sh2_bc = osb("sh2_bc", [128, D])
        g2_bc = osb("g2_bc", [128, D])
        wsh = osb("wsh", [128, 8, INC], BF16)

        sa = contextlib.ExitStack()
        with sa:
            asb, _ = mk(sa)
            w_in = wsh
            w_out = asb("w_out_bf", [128, 8, D], BF16)
            gw1_bc = asb("gw1_bc", [128, D])
            sh1_bc = asb("sh1_bc", [128, D])
            on_bc = asb("on_bc", [128, D])
            g1_bc = asb("g1_bc", [128, D])
            hc_bf = asb("hc_bf", [128, 2, D], BF16)
            kcTa = asb("kcTa", [128, 2, CTX], BF16)
            kcTb = asb("kcTb", [128, 4, CTX], BF16)
            vcaug = asb("vcaug", [128, 2, 10, 66], BF16)
            maskA = asb("maskA", [128, 2, 512], BF16)
            biasI = asb("biasI", [128, 5, 1024], BF16)
            gain_bc = asb("gain_bc", [128, 4, HD])
            esink = asb("esink", [128, 8])

            s0 = contextlib.ExitStack()
            with s0:
                sb, ps = mk(s0)
                P = Prog(nc, "s0")
                A = P.add
                cTs = sb("cTs", [128, 8, 2])
                sil = sb("sil", [128, 8, 2])
                srep = biasI[:, 3:5, :].rearrange("p w (k m) -> p w k m", k=8)
                NWM = 4
                wm = [sb("wm%d" % i, [128, 8, 512], BF16) for i in range(NWM)]
                bm = [sb("bm%d" % i, [128, 512]) for i in range(2)]
                sc1_t = sb("sc1_t", [128, D])
                csc1_t = sb("csc1_t", [128, D])
                shc_bc = sb("shc_bc", [128, D])
                gwc_bc = sb("gwc_bc", [128, D])
                n1g_bc = sb("n1g_bc", [128, D])
                cx = sb("cx", [128, D])
                cjunk = sb("cjunk", [128, D], BF16)
                cst = sb("cst", [128, 4])
                ch32 = sb("ch32", [128, D])
                pm = [ps("pm%d" % i, [128, 512]) for i in range(2)]
                pmc = [ps("pmc%d" % i, [128, 512]) for i in range(2)]
                Bn = lambda n: Buf(n)
                b_id, b_idb, b_cT, b_sil, b_srep = Bn("id"), Bn("idb"), Bn("cT"), Bn("sil"), Bn("srep")
                b_wm = [Bn("wm%d" % i) for i in range(NWM)]
                b_bm = [Bn("bm0"), Bn("bm1")]
                b_pm = [Bn("pm0"), Bn("pm1")]
                b_pmc = [Bn("pmc0"), Bn("pmc1")]
                b_tgt = {}
                A("sp", lambda e: e.dma_start(out=idf[:], in_=ident_d), writes=[b_id], dma_key="id")
                A("sp", lambda e: e.dma_start(out=cTs[:], in_=cT_d), writes=[b_cT], dma_key="cT")
                A("dve", lambda e: e.tensor_copy(idb[:], idf[:]), reads=[b_id], writes=[b_idb])
                A("act", lambda e: e.activation(sil[:], cTs[:], AF.Silu), reads=[b_cT], writes=[b_sil])
                for w in range(2):
                    A("dve", lambda e, w=w: e.tensor_copy(srep[:, w, :, :], sil[:, :, w:w + 1].to_broadcast([128, 8, 128])),
                      reads=[b_sil], writes=[b_srep])
                b_n1g = Bn("n1g")
                A("sp", lambda e: e.dma_start(out=n1g_bc[:], in_=n1g_d.partition_broadcast(128)), writes=[b_n1g], dma_key="n1g")
                b_gain, b_esink, b_on = Bn("gain"), Bn("esink"), Bn("on")
                A("sp", lambda e: e.dma_start(out=gain_bc[:].rearrange("p a d -> p (a d)"), in_=gains_d.partition_broadcast(128)),
                  writes=[b_gain], dma_key="gain")
                for a in (0, 2):
                    A("dve", lambda e, a=a: e.tensor_scalar(gain_bc[:, a, :], gain_bc[:, a, :], HD ** -0.5, None, ALU.mult),
                      reads=[b_gain], writes=[b_gain])
                A("sp", lambda e: e.dma_start(out=esink[:], in_=sink_d.partition_broadcast(128)), writes=[b_esink], dma_key="sink")
                A("act", lambda e: e.activation(esink[:], esink[:], AF.Exp), reads=[b_esink], writes=[b_esink])
                A("sp", lambda e: e.dma_start(out=on_bc[:], in_=on_d.partition_broadcast(128)), writes=[b_on], dma_key="on")
                targets = [sh1_bc, sc1_t]
                ctargets = [shc_bc, csc1_t]
                for v in range(2):
                    b_tgt[v] = Bn("tgt%d" % v)
                b_ctgt = [Bn("ctgt0"), Bn("ctgt1")]
                def issue_a_weights():
                    for k in range(8):
                        A("pool", lambda e, k=k: e.dma_start(out=w_in[:, k, :], in_=win_d[k * 128:(k + 1) * 128, :]), writes=[Bn("w_in")], dma_key="w_in")

                for ch in range(4):
                    v, half = ch // 2, ch % 2
                    sl = ch % 2
                    cs = slice(ch * 512, (ch + 1) * 512)
                    hs = slice(half * 512, (half + 1) * 512)
                    ws = ch % NWM
                    A("pool", lambda e, ws=ws, cs=cs: e.dma_start(out=wm[ws][:], in_=wmod_d[:, cs].rearrange("(k p) n -> p k n", p=128)),
                      writes=[b_wm[ws]], dma_key="wm%d" % ws)
                    A("sp", lambda e, sl=sl, cs=cs: e.dma_start(out=bm[sl][:], in_=bmod_d[:, cs].partition_broadcast(128)),
                      writes=[b_bm[sl]], dma_key="bm%d" % sl)
                    for k in range(8):
                        A("pe", lambda e, sl=sl, k=k, ch=ch: e.matmul(pm[sl][:], srep[:, 0, k, :], wm[ch % NWM][:, k, :], start=(k == 0), stop=(k == 7)),
                          reads=[b_srep, b_wm[ch % NWM]], writes=[b_pm[sl]])
                    A("dve", lambda e, sl=sl, v=v, hs=hs: e.tensor_tensor(targets[v][:, hs], pm[sl][:], bm[sl][:], ALU.add),
                      reads=[b_pm[sl], b_bm[sl]], writes=[b_tgt[v]])
                    if v < 2:
                        for k in range(8):
                            A("pe", lambda e, sl=sl, k=k, ch=ch: e.matmul(pmc[sl][:], srep[:, 1, k, :], wm[ch % NWM][:, k, :], start=(k == 0), stop=(k == 7)),
                              reads=[b_srep, b_wm[ch % NWM]], writes=[b_pmc[sl]])
                        A("dve", lambda e, sl=sl, v=v, hs=hs: e.tensor_tensor(ctargets[v][:, hs], pmc[sl][:], bm[sl][:], ALU.add),
                          reads=[b_pmc[sl], b_bm[sl]], writes=[b_ctgt[v]])
                issue_a_weights()
                b_gw1, b_gwc = Bn("gw1"), Bn("gwc")
                A("dve", lambda e: e.scalar_tensor_tensor(gw1_bc[:], sc1_t[:], 1.0, n1g_bc[:], ALU.add, ALU.mult),
                  reads=[b_tgt[1], b_n1g], writes=[b_gw1])
                A("dve", lambda e: e.scalar_tensor_tensor(gwc_bc[:], csc1_t[:], 1.0, n1g_bc[:], ALU.add, ALU.mult),
                  reads=[b_ctgt[1], b_n1g], writes=[b_gwc])
                b_cx, b_cjunk, b_cst, b_ch32, b_hc = Bn("cx"), Bn("cjunk"), Bn("cst"), Bn("ch32"), Bn("hc")
                for t in range(2):
                    A("sp", lambda e, t=t: e.dma_start(out=cx[:], in_=ctx_d[t * 128:(t + 1) * 128, :]), writes=[b_cx], dma_key="cx")
                    A("act", lambda e: e.activation(cjunk[:], cx[:], AF.Square, accum_out=cst[:, 0:1]), reads=[b_cx], writes=[b_cjunk, b_cst])
                    A("act", lambda e: e.activation(cst[:, 1:2], cst[:, 0:1], AF.Ln, bias=EPS, scale=1.0 / D), reads=[b_cst], writes=[b_cst])
                    A("act", lambda e: e.activation(cst[:, 2:3], cst[:, 1:2], AF.Exp, scale=-0.5), reads=[b_cst], writes=[b_cst])
                    A("dve", lambda e: e.scalar_tensor_tensor(ch32[:], cx[:], cst[:, 2:3], gwc_bc[:], ALU.mult, ALU.mult),
                      reads=[b_cx, b_cst, b_gwc], writes=[b_ch32])
                    A("dve", lambda e, t=t: e.tensor_tensor(hc_bf[:, t, :], ch32[:], shc_bc[:], ALU.add),
                      reads=[b_ch32, b_ctgt[0]], writes=[b_hc])
                if 's0' in phases:
                    P.emit()

            pa = contextlib.ExitStack()
            with pa:
                sb, ps = mk(pa)
                P = Prog(nc, "a")
                A = P.add
                Bn = lambda n: Buf(n)
                xt = [sb("xt%d" % i, [128, D]) for i in range(2)]
                xres = [sb("xres%d" % i, [128, D]) for i in range(2)]
                ropet = [sb("ropet%d" % i, [128, 2, 64]) for i in range(2)]
                junk = sb("junk", [128, D], BF16)
                h32 = sb("h32", [128, D])
                xn_bf = sb("xn_bf", [128, D], BF16)
                xnT = sb("xnT", [128, 8, 128], BF16)
                qk32 = sb("qk32", [128, 1664])
                T1 = sb("T1", [128, 1664])
                qk_bf = sb("qk_bf", [128, 1664], BF16)
                st = sb("st", [128, 8])
                ss26 = sb("ss26", [128, 3, 26])
                qTa = sb("qTa", [128, RQ, 4, 128], BF16)
                qTb = sb("qTb", [128, RQ, 4, 128], BF16)
                kTa = sb("kTa", [128, RK, 2, 128], BF16)
                kTb = sb("kTb", [128, RK, 4, 128], BF16)
                vaug = sb("vaug", [128, RKV, 10, 66], BF16)
                Pb = [sb("Pb%d" % i, [128, 7, 512], BF16) for i in range(2)]
                o_sb = sb("o_sb", [128, 16, 64])
                den = sb("den", [128, 2, 16])
                gst = sb("gst", [128, 8])
                m_bf = sb("m_bf", [128, D], BF16)
                mT = sb("mT", [128, 8, 128], BF16)
                biasE = sb("biasE", [128, 4, 1024], BF16)
                NPIN = int(os.environ.get("NPIN", "2"))
                NPSS = int(os.environ.get("NPSS", "2"))
                pin = [ps("pin%d" % i, [128, 512]) for i in range(NPIN)]
                ptr = [ps("ptr%d" % i, [128, 1024], BF16) for i in range(2)]
                pss = [ps("pss%d" % i, [128, 512]) for i in range(NPSS)]
                po = [ps("po%d" % i, [128, 512]) for i in range(2)]

                b_w_in, b_w_out, b_maskA, b_biasI = Bn("w_in"), Bn("w_out"), Bn("maskA"), Bn("biasI")
                b_xt = [Bn("xt0"), Bn("xt1")]
                b_xres = [Bn("xres0"), Bn("xres1")]
                b_rope = [Bn("rope0"), Bn("rope1")]
                b_junk, b_h32, b_xn, b_xnT, b_qk32, b_T1, b_qkbf, b_st, b_ss = (Bn(n) for n in
                    "junk h32 xn xnT qk32 T1 qkbf st ss".split())
                b_qTa = [Bn("qTa%d" % i) for i in range(RQ)]
                b_qTb = [Bn("qTb%d" % i) for i in range(RQ)]
                b_kTa = [Bn("kTa%d" % i) for i in range(RK)]
                b_kTb = [Bn("kTb%d" % i) for i in range(RK)]
                b_v = [Bn("v%d" % i) for i in range(RKV)]
                b_kc, b_vc = Bn("kc"), Bn("vc")
                b_Pb = [[Bn("Pb%d_%d" % (i, c)) for c in range(7)] for i in range(2)]
                b_o, b_den, b_gst, b_m, b_mT, b_biasE = (Bn(n) for n in "o den gst m mT biasE".split())
                b_pin = [Bn("pin%d" % i) for i in range(NPIN)]
                b_ptr = [Bn("ptr0"), Bn("ptr1")]
                b_pss = [Bn("pss%d" % i) for i in range(NPSS)]
                b_po = [Bn("po0"), Bn("po1")]
                b_xs = [Bn("xs%d" % i) for i in range(NT)]
                b_g1 = Bn("g1")
                b_late = {2: b_g1, 3: Bn("sh2"), 4: Bn("gw2"), 5: Bn("g2")}
                b_r2 = Bn("r2")

                A("pool", lambda e: e.memset(vaug[:], 1.0), writes=b_v)
                A("pool", lambda e: e.memset(vcaug[:], 1.0), writes=[b_vc])
                A("pool", lambda e: e.memset(kTa[:], 0.0), writes=b_kTa)
                A("pool", lambda e: e.memset(kcTa[:], 0.0), writes=[b_kc])

                cnt = {"pin": 0, "ptr": 0, "pss": 0, "po": 0, "u": 0, "ev": 0}

                def nxt(k):
                    cnt[k] += 1
                    return (cnt[k] - 1) % {"pin": NPIN, "pss": NPSS}.get(k, 2)

                def pick_eng():
                    cnt["ev"] += 1
                    return "act" if cnt["ev"] % 4 == 0 else "dve"

                def evac(out_ap, in_ap, r, w, eng=None):
                    if eng is None:
                        eng = pick_eng()
                    if eng == "act":
                        A("act", lambda e: e.copy(out_ap, in_ap), reads=r, writes=w)
                    else:
                        A("dve", lambda e: e.tensor_copy(out_ap, in_ap), reads=r, writes=w)

                def rstd(acc_ap, tmp_ap, out_ap, n, bufs):
                    A("act", lambda e: e.activation(tmp_ap, acc_ap, AF.Ln, bias=EPS, scale=1.0 / n), reads=bufs, writes=bufs, cost=0.2)
                    A("act", lambda e: e.activation(out_ap, tmp_ap, AF.Exp, scale=-0.5), reads=bufs, writes=bufs, cost=0.2)

                def transposes_and_inproj(src_bf_ap, b_src, chunks):
                    tb = nxt("ptr")
                    for k in range(8):
                        A("pe", lambda e, k=k, tb=tb: e.transpose(ptr[tb][:, k * 128:(k + 1) * 128], src_bf_ap[:, k * 128:(k + 1) * 128], idb[:]),
                          reads=[b_src], writes=[b_ptr[tb]])
                    evac(xnT[:].rearrange("p k t -> p (k t)"), ptr[tb][:], [b_ptr[tb]], [b_xnT])
                    yield
                    for (c0, c1, consumer) in chunks:
                        pb = nxt("pin")
                        for k in range(8):
                            A("pe", lambda e, k=k, pb=pb, c0=c0, c1=c1: e.matmul(pin[pb][:, 0:c1 - c0], xnT[:, k, :], w_in[:, k, c0:c1],
                                                                                start=(k == 0), stop=(k == 7)),
                              reads=[b_xnT, b_w_in], writes=[b_pin[pb]], cost=0.257)
                            if k == 3:
                                yield
                        consumer(pin[pb], b_pin[pb])
                        yield

                def qk_chain(is_ctx, rope_slot):
                    A("act", lambda e: e.activation(T1[:], qk32[:], AF.Square), reads=[b_qk32], writes=[b_T1], cost=1.5)
                    yield
                    A("dve", lambda e: e.tensor_reduce(ss26[:, 0, :], T1[:].rearrange("p (h d) -> p h d", d=HD), AX.X, ALU.add),
                      reads=[b_T1], writes=[b_ss], cost=1.8)
                    yield
                    rstd(ss26[:, 0, :], ss26[:, 1, :], ss26[:, 2, :], HD, [b_ss])
                    yield
                    A("dve", lambda e: e.tensor_tensor(T1[:].rearrange("p (h d) -> p h d", d=HD), qk32[:].rearrange("p (h d) -> p h d", d=HD),
                                                       ss26[:, 2, :].unsqueeze(2).to_broadcast([128, 26, HD]), ALU.mult),
                      reads=[b_qk32, b_ss], writes=[b_T1], cost=1.9)
                    yield

                    def gain_mul(eng, out_ap, c0, nh, gi, wbuf):
                        A(eng, lambda e: e.tensor_tensor(out_ap, T1[:, c0:c0 + nh * HD].rearrange("p (h d) -> p h d", d=HD),
                                                         gain_bc[:, gi:gi + 1, :].to_broadcast([128, nh, HD]), ALU.mult),
                          reads=[b_T1], writes=[wbuf], cost=(0.8 if eng == "dve" else 1.3) * nh / 8 + 0.1)
                    gain_mul("pool", qk_bf[:, 640:1152].rearrange("p (h d) -> p h d", d=HD), 640, 8, 2, b_qkbf)
                    gain_mul("dve", qk_bf[:, 1152:1664].rearrange("p (h d) -> p h d", d=HD), 1152, 8, 3, b_qkbf)
                    yield
                    if is_ctx:
                        gain_mul("pool", qk_bf[:, 512:640].rearrange("p (h d) -> p h d", d=HD), 512, 2, 1, b_qkbf)
                        return
                    gain_mul("pool", qk32[:, 0:512].rearrange("p (h d) -> p h d", d=HD), 0, 8, 0, b_qk32)
                    gain_mul("dve", qk32[:, 512:640].rearrange("p (h d) -> p h d", d=HD), 512, 2, 1, b_qk32)
                    yield
                    rp = ropet[rope_slot]
                    src = qk32[:, 0:640].rearrange("p (h d) -> p h d", d=HD)
                    A("dve", lambda e: e.tensor_tensor(T1[:, 0:640].rearrange("p (h d) -> p h d", d=HD), src,
                                                       rp[:, 0:1, :].to_broadcast([128, 10, HD]), ALU.mult),
                      reads=[b_qk32, b_rope[rope_slot]], writes=[b_T1])
                    yield
                    s5 = qk32[:, 0:640].rearrange("p (h a b d) -> p h a b d", a=2, b=2, d=16)
                    v5 = T1[:, 640:1280].rearrange("p (h a b d) -> p h a b d", a=2, b=2, d=16)
                    r5 = rp[:, 1, :].rearrange("p (a b d) -> p a b d", a=2, b=2, d=16)
                    for blk, eng in ((0, "pool"), (1, "dve")):
                        A(eng, lambda e, blk=blk: e.tensor_tensor(v5[:, :, :, blk, :], s5[:, :, :, 1 - blk, :],
                                                                  r5[:, :, blk, :].unsqueeze(1).to_broadcast([128, 10, 2, 16]), ALU.mult),
                          reads=[b_qk32, b_rope[rope_slot]], writes=[b_T1])
                    yield
                    A("dve", lambda e: e.tensor_tensor(qk_bf[:, 0:512].rearrange("p (j g d) -> p g j d", j=4, g=2, d=HD),
                                                       T1[:, 0:512].rearrange("p (g j d) -> p g j d", j=4, g=2, d=HD),
                                                       T1[:, 640:1152].rearrange("p (g j d) -> p g j d", j=4, g=2, d=HD), ALU.add),
                      reads=[b_T1], writes=[b_qkbf])
                    A("pool", lambda e: e.tensor_tensor(qk_bf[:, 512:640], T1[:, 512:640], T1[:, 1152:1280], ALU.add),
                      reads=[b_T1], writes=[b_qkbf])

                def qk_transposes(dst_qa, dst_ka, dst_qb, dst_kb, wq, wk):
                    tb = nxt("ptr")
                    for j in range(5):
                        if dst_qa is None and j < 4:
                            continue
                        A("pe", lambda e, j=j, tb=tb: e.transpose(ptr[tb][:, j * 128:(j + 1) * 128], qk_bf[:, j * 128:(j + 1) * 128], idb[:]),
                          reads=[b_qkbf], writes=[b_ptr[tb]])
                    en = pick_eng()
                    if dst_qa is not None:
                        evac(dst_qa, ptr[tb][:, 0:512], [b_ptr[tb]], wq, en)
                    for g_ in range(2):
                        evac(dst_ka[g_ * 64:(g_ + 1) * 64, g_, :], ptr[tb][g_ * 64:(g_ + 1) * 64, 512:640], [b_ptr[tb]], wk, en)
                    yield
                    tb = nxt("ptr")
                    for j in range(8):
                        if dst_qb is None and j < 4:
                            continue
                        A("pe", lambda e, j=j, tb=tb: e.transpose(ptr[tb][:, j * 128:(j + 1) * 128], qk_bf[:, 640 + j * 128:640 + (j + 1) * 128], idb[:]),
                          reads=[b_qkbf], writes=[b_ptr[tb]])
                    en = pick_eng()
                    if dst_qb is not None:
                        evac(dst_qb, ptr[tb][:, 0:512], [b_ptr[tb]], wq, en)
                    srckb = ptr[tb][:, 512:1024]
                    if len(dst_kb.shape) == 3:
                        srckb = srckb.rearrange("p (j t) -> p j t", j=4)
                    evac(dst_kb, srckb, [b_ptr[tb]], wk, en)
                    yield

                def inproj_consumers(v_dst, b_vdst, with_q):
                    def c0(pt, bpt):
                        evac(qk32[:, 0:512], pt[:, 0:512], [bpt], [b_qk32])

                    def c1(pt, bpt):
                        en = pick_eng()
                        evac(qk32[:, 512:640], pt[:, 0:128], [bpt], [b_qk32], en)
                        evac(v_dst[:, 0:2, 0:64], pt[:, 128:256].rearrange("p (h d) -> p h d", d=HD), [bpt], [b_vdst], en)
                        if with_q:
                            evac(qk32[:, 640:896], pt[:, 256:512], [bpt], [b_qk32], en)

                    def c2(pt, bpt):
                        en = pick_eng()
                        if with_q:
                            evac(qk32[:, 896:1152], pt[:, 0:256], [bpt], [b_qk32], en)
                        evac(qk32[:, 1152:1408], pt[:, 256:512], [bpt], [b_qk32], en)

                    def c3(pt, bpt):
                        en = pick_eng()
                        evac(qk32[:, 1408:1664], pt[:, 0:256], [bpt], [b_qk32], en)
                        evac(v_dst[:, 2:6, 0:64], pt[:, 256:512].rearrange("p (h d) -> p h d", d=HD), [bpt], [b_vdst], en)

                    def c4(pt, bpt):
                        evac(v_dst[:, 6:10, 0:64], pt[:, 0:256].rearrange("p (h d) -> p h d", d=HD), [bpt], [b_vdst])
                    ch = [(512, 1024, c1), (1024, 1536, c2), (1536, 2048, c3), (2048, 2304, c4)]
                    if with_q:
                        ch = [(0, 512, c0)] + ch
                    return ch

                A("pool", lambda e: e.dma_start(out=maskA[:], in_=maskA_d.rearrange("a p n -> p a n")), writes=[b_maskA], dma_key="maskA")
                for k in range(8):
                    A("pool", lambda e, k=k: e.dma_start(out=w_out[:, k, :], in_=wout_d[k * 128:(k + 1) * 128, :]), writes=[b_w_out], dma_key="w_out")
                A_STOP = os.environ.get("A_STOP", "")
                b_hc2 = Bn("hc2")
                for t in range(2 if A_STOP != "w" else 0):
                    if not True:
                        pass
                    if t == 0:
                        A("dve", lambda e: e.memset(qk32[:], 1.0), writes=[b_qk32])
                    for _ in transposes_and_inproj(hc_bf[:, t, :], b_hc2, inproj_consumers(vcaug[:, t], b_vc, False)):
                        pass
                    for _ in qk_chain(True, 0):
                        pass
                    for _ in qk_transposes(None, kcTa[:, :, t * 128:(t + 1) * 128], None,
                                           kcTb[:, :, t * 128:(t + 1) * 128], None, [b_kc]):
                        pass

                def load_x(i):
                    s = i % 2
                    A("sp", lambda e: e.dma_start(out=xt[s][:], in_=x_d[i * 128:(i + 1) * 128, :]), writes=[b_xt[s]], dma_key="xt%d" % s)

                def load_rope(i):
                    s = i % 2
                    A("sp", lambda e: e.dma_start(out=ropet[s][:], in_=rope_d[i]), writes=[b_rope[s]], dma_key="rope%d" % s)

                def stage_norm(i):
                    s = i % 2
                    A("act", lambda e: e.activation(junk[:], xt[s][:], AF.Square, accum_out=st[:, 0:1]), reads=[b_xt[s]], writes=[b_junk, b_st], cost=1.06)
                    rstd(st[:, 0:1], st[:, 1:2], st[:, 2:3], D, [b_st])
                    yield
                    A("dve", lambda e: e.scalar_tensor_tensor(h32[:], xt[s][:], st[:, 2:3], gw1_bc[:], ALU.mult, ALU.mult),
                      reads=[b_xt[s], b_st], writes=[b_h32], cost=1.2)
                    yield
                    A("dve", lambda e: e.tensor_tensor(xn_bf[:], h32[:], sh1_bc[:], ALU.add), reads=[b_h32], writes=[b_xn], cost=1.26)
                    yield

                def stage_inproj(i):
                    yield from transposes_and_inproj(xn_bf[:], b_xn, inproj_consumers(vaug[:, i % RKV], b_v[i % RKV], True))

                def stage_qkchain(i):
                    yield from qk_chain(False, i % 2)

                def stage_qkT(i):
                    qs, ks = i % RQ, i % RK
                    yield from qk_transposes(qTa[:, qs].rearrange("p j t -> p (j t)"), kTa[:, ks],
                                             qTb[:, qs].rearrange("p j t -> p (j t)"), kTb[:, ks].rearrange("p j t -> p (j t)"),
                                             [b_qTa[qs], b_qTb[qs]], [b_kTa[ks], b_kTb[ks]])

                def ksrc(kind, idx):
                    if kind == "ctx":
                        return (kcTa[:, :, idx * 128:(idx + 1) * 128], kcTb[:, :, idx * 128:(idx + 1) * 128], vcaug[:, idx], [b_kc], [b_vc])
                    s = idx % RK
                    sv = idx % RKV
                    return (kTa[:, s], kTb[:, s], vaug[:, sv], [b_kTa[s], b_kTb[s]], [b_v[sv]])

                def units_for(n):
                    qs = n % RQ
                    a_chunks = []
                    for t, mk_ in ((n - 1, 0), (n, None), (n + 1, 1)):
                        if 0 <= t < NT:
                            a_chunks.append(("tile", t, mk_))
                    a_chunks += [("ctx", 0, None), ("ctx", 1, None)]
                    b_chunks = [("tile", t, _bias_id(n, t)) for t in _b_tiles(n)] + [("ctx", 0, None), ("ctx", 1, None)]
                    us = []
                    for g in range(2):
                        us.append(dict(kind="A", g=g, chunks=a_chunks, qs=qs, n=n))
                    for par in range(2):
                        us.append(dict(kind="B", g=par, chunks=b_chunks, qs=qs, n=n))
                    return us

                def unit_qk(u, pbi):
                    base = u["g"] * 64
                    qs = u["qs"]
                    n = u["n"]
                    for ci, (kind, idx, extra) in enumerate(u["chunks"]):
                        ka, kb, _, kbufs, _ = ksrc(kind, idx)
                        sbk = nxt("pss")
                        pst = pss[sbk]
                        if u["kind"] == "A":
                            if extra is not None:
                                A("pe", lambda e, pst=pst, extra=extra: e.matmul(pst[:], idb[:], maskA[:, extra, :], start=True, stop=False),
                                  reads=[b_maskA], writes=[b_pss[sbk]], cost=0.257)
                            A("pe", lambda e, pst=pst, ka=ka, extra=extra: e.matmul(
                                pst[:], ka[:, u["g"], :], qTa[:, qs].rearrange("p j t -> p (j t)"),
                                start=(extra is None), stop=True),
                              reads=kbufs + [b_qTa[qs]], writes=[b_pss[sbk]], cost=0.26)
                        else:
                            if extra is not None:
                                if n in EDGE_BLOCKS:
                                    e_i = extra - (5 + 4 * EDGE_BLOCKS.index(n))
                                    bt, bbuf = biasE[:, e_i, :], b_biasE
                                else:
                                    bt, bbuf = biasI[:, extra, :], b_biasI
                                bsel = bt.rearrange("p (pr two i) -> p pr two i", pr=4, two=2, i=128)[:, :, u["g"], :]
                                A("pe", lambda e, pst=pst, bsel=bsel: e.matmul(pst[:].rearrange("p (pr i) -> p pr i", pr=4), idb[:], bsel,
                                                                               start=True, stop=False),
                                  reads=[bbuf], writes=[b_pss[sbk]], cost=0.257)
                            for pr in range(4):
                                A("pe", lambda e, pst=pst, kb=kb, pr=pr, extra=extra: e.matmul(
                                    pst[:, pr * 128:(pr + 1) * 128], kb[base:base + 64, pr, :], qTb[base:base + 64, qs, pr, :],
                                    start=(extra is None), stop=(True if extra is None else pr == 3)),
                                  reads=kbufs + [b_qTb[qs]], writes=[b_pss[sbk]], cost=0.083)
                        A("act", lambda e, pst=pst, ci=ci: e.activation(Pb[pbi][:, ci, :], pst[:], AF.Exp), reads=[b_pss[sbk]], writes=[b_Pb[pbi][ci]])
                        yield

                def unit_pv(u, pbi):
                    pob = nxt("po")
                    pot = po[pob][:, 0:260].rearrange("p (h d) -> p h d", d=65)
                    nchunk = len(u["chunks"])
                    for j in range(4):
                        for ci, (kind, idx, extra) in enumerate(u["chunks"]):
                            _, _, v, _, vbufs = ksrc(kind, idx)
                            hv = u["g"] if u["kind"] == "A" else 2 + 2 * j + u["g"]
                            A("pe", lambda e, j=j, ci=ci, v=v, hv=hv: e.matmul(pot[:, j, :], Pb[pbi][:, ci, j * 128:(j + 1) * 128], v[:, hv, 0:65],
                                                                                 start=(ci == 0), stop=(ci == nchunk - 1)),
                              reads=[b_Pb[pbi][ci]] + vbufs, writes=[b_po[pob]], cost=0.056)
                        yield
                    if u["kind"] == "A":
                        heads = [4 * u["g"] + j for j in range(4)]
                        hsl = slice(4 * u["g"], 4 * u["g"] + 4)
                        A("dve", lambda e: e.tensor_tensor(den[:, 0, hsl], pot[:, :, 64], esink[:, hsl], ALU.add), reads=[b_po[pob]], writes=[b_den], cost=0.1)
                        A("dve", lambda e: e.reciprocal(den[:, 1, hsl], den[:, 0, hsl]), reads=[b_den], writes=[b_den], cost=0.18)
                        A("dve", lambda e: e.tensor_tensor(o_sb[:, hsl, :], pot[:, :, 0:64], den[:, 1, hsl].unsqueeze(2).to_broadcast([128, 4, 64]), ALU.mult),
                          reads=[b_po[pob], b_den], writes=[b_o])
                    else:
                        par = u["g"]
                        dv = den[:, :, 8:16].rearrange("p a (pr two) -> p a pr two", two=2)
                        A("dve", lambda e: e.reciprocal(dv[:, 1, :, par], pot[:, :, 64]), reads=[b_po[pob]], writes=[b_den], cost=0.18)
                        ov = o_sb[:, 8:16, :].rearrange("p (pr two) d -> p pr two d", two=2)
                        A("dve", lambda e: e.tensor_tensor(ov[:, :, par, :], pot[:, :, 0:64], dv[:, 1, :, par].unsqueeze(2).to_broadcast([128, 4, 64]), ALU.mult),
                          reads=[b_po[pob], b_den], writes=[b_o])
                    yield

                def merge_prep(n):
                    for g in range(2):
                        osl = o_sb[:, 8 * g:8 * g + 8, :].rearrange("p h d -> p (h d)")
                        A("act", lambda e, g=g, osl=osl: e.activation(junk[:, 0:512], osl, AF.Square, accum_out=gst[:, g:g + 1]),
                          reads=[b_o], writes=[b_junk, b_gst])
                    yield
                    rstd(gst[:, 0:2], gst[:, 2:4], gst[:, 4:6], 512, [b_gst])
                    yield
                    for g, eng in ((0, "dve"), (1, "dve")):
                        osl = o_sb[:, 8 * g:8 * g + 8, :].rearrange("p h d -> p (h d)")
                        A(eng, lambda e, g=g, osl=osl: e.scalar_tensor_tensor(m_bf[:, g * 512:(g + 1) * 512], osl, gst[:, 4 + g:5 + g],
                                                                              on_bc[:, g * 512:(g + 1) * 512], ALU.mult, ALU.mult),
                          reads=[b_o, b_gst], writes=[b_m], cost=0.75)
                        yield

                def out_proj(n):
                    s = n % 2
                    tb = nxt("ptr")
                    for k in range(8):
                        A("pe", lambda e, k=k, tb=tb: e.transpose(ptr[tb][:, k * 128:(k + 1) * 128], m_bf[:, k * 128:(k + 1) * 128], idb[:]),
                          reads=[b_m], writes=[b_ptr[tb]])
                    evac(mT[:].rearrange("p k t -> p (k t)"), ptr[tb][:], [b_ptr[tb]], [b_mT])
                    yield
                    for nch in range(2):
                        pb = nxt("pin")
                        for k in range(8):
                            A("pe", lambda e, k=k, pb=pb, nch=nch: e.matmul(pin[pb][:], mT[:, k, :], w_out[:, k, nch * 512:(nch + 1) * 512],
                                                                             start=(k == 0), stop=(k == 7)),
                              reads=[b_mT, b_w_out], writes=[b_pin[pb]], cost=0.257)
                            if k == 3:
                                yield
                        yield
                        cs = slice(nch * 512, (nch + 1) * 512)
                        A("dve", lambda e, pb=pb, cs=cs: e.tensor_tensor(h32[:, cs], pin[pb][:], g1_bc[:, cs], ALU.mult),
                          reads=[b_pin[pb], b_g1], writes=[b_h32], cost=0.7)
                        A("pool", lambda e, cs=cs: e.tensor_tensor(xres[s][:, cs], xres[s][:, cs], h32[:, cs], ALU.add),
                          reads=[b_h32, b_xres[s]], writes=[b_xres[s]], cost=1.27)
                        yield
                    A("act", lambda e: e.activation(junk[:], xres[s][:], AF.Square, accum_out=gst[:, 6:7]), reads=[b_xres[s]], writes=[b_junk, b_gst], cost=0.95)
                    A("act", lambda e: e.activation(gst[:, 7:8], gst[:, 6:7], AF.Ln, bias=EPS, scale=1.0 / D), reads=[b_gst], writes=[b_gst], cost=0.2)
                    A("act", lambda e: e.activation(r2all[:, n:n + 1], gst[:, 7:8], AF.Exp, scale=-0.5), reads=[b_gst], writes=[b_r2], cost=0.2)
                    A("sp", lambda e: e.dma_start(out=xs_d[n * 128:(n + 1) * 128, :], in_=xres[s][:]), reads=[b_xres[s]], writes=[b_xs[n]],
                      dma_key="xs%d" % s)
                    yield

                pending = []

                def attn_gen(n):
                    for u in units_for(n):
                        pbi = nxt("u")
                        yield from unit_qk(u, pbi)
                        if pending:
                            yield from unit_pv(*pending.pop(0))
                        pending.append((u, pbi))
                    yield from unit_pv(*pending.pop(0))

                def chain_gens(gens):
                    for g in gens:
                        yield from g

                def run_all(g):
                    for _ in g:
                        pass

                def emit_late_mod():
                    srepA = biasI[:, 3, :].rearrange("p (k m) -> p k m", k=8)
                    late_tgt = {2: g1_bc, 3: sh2_bc, 4: gw2_bc, 5: g2_bc}
                    o_flat = o_sb[:].rearrange("p h d -> p (h d)")
                    A("sp", lambda e: e.dma_start(out=xres[0][:], in_=n2g_d.partition_broadcast(128)), writes=[b_xres[0]], dma_key="n2g")
                    for cidx in range(16):
                        v, q_ = 2 + cidx // 4, cidx % 4
                        col0 = v * D + q_ * 256
                        i_ = cidx % 2
                        stg = Pb[i_][:, 0:4, :].rearrange("p a (b c) -> p (a b) c", b=2, c=256)
                        bmst = o_flat[:, i_ * 256:(i_ + 1) * 256]
                        A("pool", lambda e, stg=stg, col0=col0: e.dma_start(out=stg, in_=wmod_d[:, col0:col0 + 256].rearrange("(k p) n -> p k n", p=128)),
                          writes=b_Pb[i_][0:4], dma_key="lm%d" % i_)
                        A("sp", lambda e, bmst=bmst, col0=col0: e.dma_start(out=bmst, in_=bmod_d[:, col0:col0 + 256].partition_broadcast(128)),
                          writes=[b_o], dma_key="lb%d" % i_)
                        pb = nxt("pin")
                        for k in range(8):
                            A("pe", lambda e, k=k, pb=pb, stg=stg: e.matmul(pin[pb][:, 0:256], srepA[:, k, :], stg[:, k, :], start=(k == 0), stop=(k == 7)),
                              reads=[b_biasI] + b_Pb[i_][0:4], writes=[b_pin[pb]], cost=0.12)
                        tg = late_tgt[v][:, q_ * 256:(q_ + 1) * 256]
                        A("dve", lambda e, tg=tg, pb=pb, bmst=bmst: e.tensor_tensor(tg, pin[pb][:, 0:256], bmst, ALU.add),
                          reads=[b_pin[pb], b_o], writes=[b_late[v]])
                    A("dve", lambda e: e.scalar_tensor_tensor(gw2_bc[:], gw2_bc[:], 1.0, xres[0][:], ALU.add, ALU.mult),
                      reads=[b_late[4], b_xres[0]], writes=[b_late[4]])

                LAG = 4
                load_x(0)
                load_x(1)
                load_rope(0)
                for it in range(-1, NT + LAG + 1):
                    n = it - LAG
                    if it == NT:
                        for c0_, c1_ in ((0, 512), (512, 1024), (1024, 1536), (1536, 2048), (2048, INC)):
                            A("pool", lambda e, c0_=c0_, c1_=c1_: e.dma_start(out=wsh[:, :, c0_:c1_], in_=wg_d[:, c0_:c1_].rearrange("(k p) n -> p k n", p=128)),
                              writes=[b_w_in], dma_key="wgpre")
                    if it == 1:
                        emit_late_mod()
                        A("pool", lambda e: e.dma_start(out=biasI[:], in_=biasB_d[0:5].rearrange("a p n -> p a n")), writes=[b_biasI], dma_key="biasI")
                    if 0 <= it + 2 < NT and it >= 0:
                        load_x(it + 2)
                    if 0 <= it + 1 < NT:
                        load_rope(it + 1)
                    if 0 <= n < NT:
                        s = n % 2
                        A("sp", lambda e, n=n, s=s: e.dma_start(out=xres[s][:], in_=x_d[n * 128:(n + 1) * 128, :]), writes=[b_xres[s]],
                          dma_key="xres%d" % s)
                        if n in EDGE_BLOCKS:
                            e0 = 5 + 4 * EDGE_BLOCKS.index(n)
                            A("pool", lambda e, e0=e0: e.dma_start(out=biasE[:], in_=biasB_d[e0:e0 + 4].rearrange("a p n -> p a n")),
                              writes=[b_biasE], dma_key="biasE")
                    pre = []
                    if 0 <= n - 1 < NT:
                        pre.append(merge_prep(n - 1))
                    if 0 <= it < NT:
                        pre.append(stage_inproj(it))
                    if 1 <= it <= NT:
                        pre.append(stage_qkT(it - 1))
                    rest = []
                    if 0 <= n - 1 < NT:
                        rest.append(out_proj(n - 1))
                    if 0 <= it < NT:
                        rest.append(stage_qkchain(it))
                    if 0 <= it + 1 < NT:
                        rest.append(stage_norm(it + 1))
                    if not (0 <= n < NT):
                        run_all(chain_gens(pre + rest))
                        continue
                    if n == 0:
                        run_all(chain_gens(pre))
                        fill = chain_gens(rest)
                    else:
                        fill = chain_gens(pre + rest)
                    for _ in attn_gen(n):
                        next(fill, None)
                    run_all(fill)
                if 'a' in phases:
                    P.emit(window=int(os.environ.get("A_WINDOW", "64")))
                    print("phase A estimate us:", P.est_time)

        sbk = contextlib.ExitStack()
        with sbk:
            sb, ps = mk(sbk)
            P = Prog(nc, "b")
            A = P.add
            Bn = lambda n: Buf(n)
            wgB = sb("wgB", [128, 8, FH - INC], BF16)
            wu = sb("wu", [128, 8, FH], BF16)
            wd = sb("wd", [128, NJ, D], BF16)
            xin = [sb("xin%d" % i, [128, D]) for i in range(2)]
            xrs = [sb("xrs%d" % i, [128, D]) for i in range(2)]
            tmp = sb("tmpb", [128, 512])
            xn2 = sb("xn2", [128, D], BF16)
            xn2T = [sb("xn2T%d" % i, [128, 8, 512], BF16) for i in range(2)]
            actT = sb("actT", [128, NJ, 512], BF16)
            sg = sb("sg", [128, 512])
            pg = [ps("pg%d" % i, [128, 512]) for i in range(2)]
            pu = [ps("pu%d" % i, [128, 512]) for i in range(2)]
            pd = [ps("pd%d" % i, [128, 512]) for i in range(2)]
            ptr = [ps("ptrb%d" % i, [128, 1024], BF16) for i in range(2)]
            NG = (NJ + 3) // 4
            b_wg = [Bn("wg%d" % g) for g in range(NG)]
            b_wu = [Bn("wu%d" % g) for g in range(NG)]
            b_wd = [Bn("wd%d" % g) for g in range(NG)]
            b_xin = [Bn("xin0"), Bn("xin1")]
            b_xrs = [Bn("xrs0"), Bn("xrs1")]
            b_tmp, b_xn2, b_sg = Bn("tmp"), Bn("xn2"), Bn("sg")
            b_xn2T = [Bn("xn2T0"), Bn("xn2T1")]
            b_actT = [Bn("actT%d" % j) for j in range(NJ)]
            b_pg = [Bn("pg0"), Bn("pg1")]
            b_pu = [Bn("pu0"), Bn("pu1")]
            b_pd = [Bn("pd0"), Bn("pd1")]
            b_ptr = [Bn("ptr0"), Bn("ptr1")]
            b_out = [Bn("out%d" % i) for i in range(NT)]
            cnt = {"ptr": 0, "g": 0, "pd": 0}

            def nxt(k):
                cnt[k] += 1
                return (cnt[k] - 1) % 2

            def fold_wd(g):
                j0, j1 = g * 4, min(g * 4 + 4, NJ)
                A("dve", lambda e: e.tensor_tensor(wd[:, j0:j1, :], wd[:, j0:j1, :],
                                                   g2_bc[:].unsqueeze(1).to_broadcast([128, j1 - j0, D]), ALU.mult),
                  reads=[b_wd[g]], writes=[b_wd[g]])

            for g in range(NG):
                c0, c1 = g * 512, min((g + 1) * 512, FH)
                if g == NG - 1:
                    A("pool", lambda e: e.dma_start(out=wgB[:], in_=wg_d[:, INC:FH].rearrange("(k p) n -> p k n", p=128)),
                      writes=[b_wg[g]], dma_key="wg%d" % g)
                A("pool", lambda e, c0=c0, c1=c1: e.dma_start(out=wu[:, :, c0:c1], in_=wu_d[:, c0:c1].rearrange("(k p) n -> p k n", p=128)),
                  writes=[b_wu[g]], dma_key="wu%d" % g)
                j0, j1 = g * 4, min(g * 4 + 4, NJ)
                A("pool", lambda e, j0=j0, j1=j1: e.dma_start(out=wd[:, j0:j1, :], in_=wd_d[j0 * 128:j1 * 128, :].rearrange("(j p) n -> p j n", p=128)),
                  writes=[b_wd[g]], dma_key="wd%d" % g)

            def prep_gen(blk):
                bs = blk % 2
                for tt in range(4):
                    t = blk * 4 + tt
                    s = t % 2
                    A("sp", lambda e, t=t, s=s: e.dma_start(out=xin[s][:], in_=xs_d[t * 128:(t + 1) * 128, :]), writes=[b_xin[s]], dma_key="xin%d" % s)
                    for half in range(2):
                        cs = slice(half * 512, (half + 1) * 512)
                        A("dve", lambda e, t=t, s=s, cs=cs: e.scalar_tensor_tensor(tmp[:], xin[s][:, cs], r2all[:, t:t + 1], gw2_bc[:, cs], ALU.mult, ALU.mult),
                          reads=[b_xin[s]], writes=[b_tmp])
                        yield
                        A("dve", lambda e, cs=cs: e.tensor_tensor(xn2[:, cs], tmp[:], sh2_bc[:, cs], ALU.add), reads=[b_tmp], writes=[b_xn2])
                        yield
                    tb = nxt("ptr")
                    for k in range(8):
                        A("pe", lambda e, k=k, tb=tb: e.transpose(ptr[tb][:, k * 128:(k + 1) * 128], xn2[:, k * 128:(k + 1) * 128], idb[:]),
                          reads=[b_xn2], writes=[b_ptr[tb]])
                    yield
                    A("act", lambda e, tb=tb, tt=tt, bs=bs: e.copy(xn2T[bs][:, :, tt * 128:(tt + 1) * 128], ptr[tb][:].rearrange("p (k t) -> p k t", k=8)),
                      reads=[b_ptr[tb]], writes=[b_xn2T[bs]])
                    yield

            def gateup_gen(blk):
                bs = blk % 2
                for j in range(NJ):
                    gb = nxt("g")
                    gi = j // 4
                    for k in range(8):
                        wsl = wsh[:, k, j * 128:(j + 1) * 128] if j < INC // 128 else wgB[:, k, (j - INC // 128) * 128:(j - INC // 128 + 1) * 128]
                        A("pe", lambda e, k=k, gb=gb, wsl=wsl: e.matmul(pg[gb][:], wsl, xn2T[bs][:, k, :], start=(k == 0), stop=(k == 7)),
                          reads=[b_wg[NG - 1 if j >= INC // 128 else gi], b_xn2T[bs]], writes=[b_pg[gb]])
                    for k in range(8):
                        A("pe", lambda e, k=k, j=j, gb=gb: e.matmul(pu[gb][:], wu[:, k, j * 128:(j + 1) * 128], xn2T[bs][:, k, :], start=(k == 0), stop=(k == 7)),
                          reads=[b_wu[gi], b_xn2T[bs]], writes=[b_pu[gb]])
                    A("act", lambda e, gb=gb: e.activation(sg[:], pg[gb][:], AF.Silu), reads=[b_pg[gb]], writes=[b_sg])
                    A("dve", lambda e, gb=gb, j=j: e.tensor_tensor(actT[:, j, :], pu[gb][:], sg[:], ALU.mult),
                      reads=[b_pu[gb], b_sg], writes=[b_actT[j]])
                    if blk == 0:
                        for g in range(NG):
                            if j == min(4 * g + 11, NJ - 1) or (j == NJ - 1 and 4 * g + 11 > NJ - 1):
                                fold_wd(g)
                    yield

            def down(blk):
                for tt in range(4):
                    t = blk * 4 + tt
                    s = t % 2
                    A("sp", lambda e, t=t, s=s: e.dma_start(out=xrs[s][:], in_=xs_d[t * 128:(t + 1) * 128, :]), writes=[b_xrs[s]], dma_key="xrs%d" % s)
                    for nch in range(2):
                        db = nxt("pd")
                        cs = slice(nch * 512, (nch + 1) * 512)
                        for j in range(NJ):
                            A("pe", lambda e, j=j, db=db, tt=tt, cs=cs: e.matmul(pd[db][:], actT[:, j, tt * 128:(tt + 1) * 128], wd[:, j, cs],
                                                                                  start=(j == 0), stop=(j == NJ - 1)),
                              reads=[b_actT[j], b_wd[j // 4]], writes=[b_pd[db]])
                        A("dve", lambda e, db=db, cs=cs, s=s: e.tensor_tensor(xrs[s][:, cs], pd[db][:], xrs[s][:, cs], ALU.add),
                          reads=[b_pd[db], b_xrs[s]], writes=[b_xrs[s]])
                    A("sp", lambda e, t=t, s=s: e.dma_start(out=out_d[t * 128:(t + 1) * 128, :], in_=xrs[s][:]), reads=[b_xrs[s]], writes=[b_out[t]],
                      dma_key="o%d" % s)

            for _ in prep_gen(0):
                pass
            for blk in range(NT // 4):
                fill = prep_gen(blk + 1) if blk + 1 < NT // 4 else iter(())
                for _ in gateup_gen(blk):
                    next(fill, None)
                for _ in fill:
                    pass
                down(blk)
            if 'b' in phases:
                P.emit()
    return nc, dbg_outs


_NC_CACHE = {}


def _prep_inputs(inputs, b):
    f = lambda a: np.ascontiguousarray(np.asarray(a, dtype=np.float32))
    c = f(inputs["c"])[b]
    c_ctx = f(inputs["c_ctx"])
    cT = np.stack([c.reshape(8, 128).T, c_ctx.reshape(8, 128).T], -1)
    gains = np.concatenate([f(inputs["qn_a"])[0], f(inputs["kn_a"])[0], f(inputs["qn_b"])[0], f(inputs["kn_b"])[0]])[None, :]
    on = np.concatenate([f(inputs["on_a"])[0], f(inputs["on_b"])[0]])[None, :]
    return {
        "x": f(inputs["x"])[b], "ctx": f(inputs["ctx"])[b], "cT": np.ascontiguousarray(cT),
        "w_mod": f(inputs["w_mod"])[0], "b_mod": f(inputs["b_mod"])[0][None, :],
        "n1g": f(inputs["norm1_g"])[0][None, :], "n2g": f(inputs["norm2_g"])[0][None, :],
        "w_in": f(inputs["w_in"])[0], "gains": np.ascontiguousarray(gains), "sink": f(inputs["sink_a"])[0][None, :],
        "on": np.ascontiguousarray(on), "w_out": f(inputs["w_out"])[0],
        "w_gate": f(inputs["w_gate"])[0], "w_up": f(inputs["w_up"])[0], "w_down": f(inputs["w_down"])[0],
    }


def kernel(**inputs):
    if "nc" not in _NC_CACHE:
        _NC_CACHE["nc"] = build(False)[0]
    nc = _NC_CACHE["nc"]
    bias, rope, maskA = _host_consts(np.asarray(inputs["rpb_b"], dtype=np.float32)[0])
    consts = {"ident": np.eye(128, dtype=np.float32), "rope": rope, "maskA": maskA, "biasB": bias}
    in_maps = []
    for b in range(8):
        m = _prep_inputs(inputs, b)
        m.update(consts)
        in_maps.append(m)
    res = run_bass_kernel_spmd(nc, in_maps, core_ids=list(range(8)))
    return np.stack([np.asarray(r["out"], dtype=np.float32) for r in res.results], 0)
```

```python
import contextlib
import os
import numpy as np
import concourse.bass as bass
import concourse.mybir as mybir
from concourse.bass_utils import run_bass_kernel_spmd

F32 = mybir.dt.float32
BF16 = mybir.dt.bfloat16
AF = mybir.ActivationFunctionType
ALU = mybir.AluOpType
AX = mybir.AxisListType

S, D, CTX, HD = 4096, 1024, 256, 64
NT = S // 128
INC = 2304
FH = 2816
NJ = FH // 128
EPS = 1e-6
NEG = -30000.0
RK = 7
RKV = 8
RQ = 4
SAME_ENGINE_SYNC = True
FULL_SAME_ENGINE = True


class Buf:
    __slots__ = ("name", "writers", "readers")

    def __init__(self, name):
        self.name = name
        self.writers = []
        self.readers = []


class Op:
    __slots__ = ("eng", "fn", "deps", "raw", "dma_key", "idx", "signal", "tick", "_need", "cost", "pos")

    def __init__(self, eng, fn, dma_key, idx, cost=None):
        self.cost = cost
        self.pos = idx
        self.eng = eng
        self.fn = fn
        self.deps = set()
        self.raw = set()
        self.dma_key = dma_key
        self.idx = idx
        self.signal = False
        self.tick = None


class Prog:
    ENGS = ("pe", "act", "dve", "pool", "sp")

    def __init__(self, nc, tag):
        self.nc = nc
        self.ops = []
        self.tag = tag

    DEF_COST = {"pe": 0.08, "act": 0.55, "dve": 0.5, "pool": 1.0, "sp": 0.05}

    def add(self, eng, fn, reads=(), writes=(), dma_key=None, cost=None):
        if cost is None:
            cost = 4.0 if dma_key is not None else self.DEF_COST[eng]
        op = Op(eng, fn, dma_key, len(self.ops), cost)
        for b in reads:
            op.deps.update(b.writers)
            op.raw.update(b.writers)
        for b in writes:
            op.deps.update(b.readers)
            op.deps.update(b.writers)
        for b in reads:
            b.readers.append(op.idx)
        for b in writes:
            if b.readers:
                b.writers = [op.idx]
                b.readers = []
            else:
                b.writers.append(op.idx)
        op.deps.discard(op.idx)
        self.ops.append(op)
        return op

    def _needs_wait(self, op, d):
        if d.dma_key is not None:
            return True
        if d.eng != op.eng:
            return True
        if op.eng in ("pe", "sp"):
            return False
        return SAME_ENGINE_SYNC and op.dma_key is None and (d.idx in op.raw or FULL_SAME_ENGINE)

    def schedule(self, window):
        ops = self.ops
        LAT = 0.15
        finish = [None] * len(ops)
        tail = [0.0] * len(ops)
        for op in reversed(ops):
            tl = tail[op.idx] + op.cost
            tail[op.idx] = tl
            for di in op.deps:
                if tail[di] < tl:
                    tail[di] = tl
        use_tail = os.environ.get("SCHED_TAIL", "1") == "1"
        remaining = {e: [o.idx for o in ops if o.eng == e] for e in self.ENGS}
        t_eng = {e: 0.0 for e in self.ENGS}
        order = {e: [] for e in self.ENGS}
        left = len(ops)
        while left:
            best = None
            for e in self.ENGS:
                lst = remaining[e]
                te = t_eng[e]
                for pos in range(min(window, len(lst))):
                    op = ops[lst[pos]]
                    ready = te
                    ok = True
                    for di in op.deps:
                        f = finish[di]
                        if f is None:
                            ok = False
                            break
                        d = ops[di]
                        if d.eng != e or d.dma_key is not None:
                            f += LAT
                        if f > ready:
                            ready = f
                    if not ok:
                        continue
                    key = (ready, -tail[op.idx], op.idx) if use_tail else (ready, op.idx)
                    if best is None or key < best[0]:
                        best = (key, e, pos, ready)
                    if ready <= te and not use_tail:
                        break
            oi = best[0][-1]
            ready = best[0][0]
            e, pos = best[1], best[2]
            op = ops[oi]
            if op.dma_key is not None:
                t_eng[e] = ready + (1.0 if e == "pool" else 0.06)
                finish[oi] = ready + op.cost
            else:
                t_eng[e] = ready + op.cost
                finish[oi] = t_eng[e]
            remaining[e].pop(pos)
            op.pos = len(order[e])
            order[e].append(oi)
            left -= 1
        self.est_time = max(f for f in finish if f is not None)
        return order

    def emit(self, window=None):
        nc = self.nc
        ops = self.ops
        if window:
            order = self.schedule(window)
        else:
            order = {e: [o.idx for o in ops if o.eng == e] for e in self.ENGS}
            for e in self.ENGS:
                for p_, oi in enumerate(order[e]):
                    ops[oi].pos = p_
        for op in ops:
            if op.dma_key is not None:
                op.signal = True
            best = {}
            for di in op.deps:
                d = ops[di]
                if d.dma_key is None and self._needs_wait(op, d):
                    if d.eng not in best or ops[best[d.eng]].pos < d.pos:
                        best[d.eng] = di
            op._need = best
            for di in best.values():
                ops[di].signal = True
        last_dma = {}
        for e in self.ENGS:
            for oi in order[e]:
                if ops[oi].dma_key is not None:
                    last_dma[ops[oi].dma_key] = oi
        fin = Op("sp", lambda e: None, None, len(ops), 0.0)
        fin.deps = set(last_dma.values())
        fin._need = {}
        ops.append(fin)
        order["sp"].append(fin.idx)
        cnt = {}
        per_eng = {e: [ops[oi] for oi in order[e]] for e in self.ENGS}
        for e in self.ENGS:
            for op in per_eng[e]:
                if not op.signal:
                    continue
                key = ("dma", op.dma_key) if op.dma_key is not None else ("eng", op.eng)
                cnt[key] = cnt.get(key, 0) + (16 if op.dma_key is not None else 1)
                op.tick = (key, cnt[key])
        with contextlib.ExitStack() as st:
            sems = {}
            for n, k in enumerate(sorted(cnt.keys())):
                sems[k] = st.enter_context(nc.semaphore("s%s_%d" % (self.tag, n)))
            block = st.enter_context(nc.Block())

            def run(eng_name):
                def body(e):
                    seen = {}
                    for op in per_eng[eng_name]:
                        waits = {}
                        for di in op.deps:
                            d = ops[di]
                            if not self._needs_wait(op, d):
                                continue
                            if d.dma_key is None and op._need.get(d.eng) != di:
                                continue
                            k, v = d.tick
                            if seen.get(k, 0) >= v:
                                continue
                            if waits.get(k, 0) < v:
                                waits[k] = v
                        for k, v in waits.items():
                            e.wait_ge(sems[k], v)
                            seen[k] = v
                        inst = op.fn(e)
                        if op.signal:
                            inst.then_inc(sems[op.tick[0]], 16 if op.dma_key is not None else 1)
                return body

            if per_eng["pe"]:
                block.tensor(run("pe"))
            if per_eng["act"]:
                block.scalar(run("act"))
            if per_eng["dve"]:
                block.vector(run("dve"))
            if per_eng["pool"]:
                block.gpsimd(run("pool"))
            block.sync(run("sp"))
        return cnt


def _na_vis(n, t):
    i = np.arange(128)
    q_r = 2 * n + i // 64
    q_c = i % 64
    j = np.arange(128)
    k_r = 2 * t + j // 64
    k_c = j % 64
    rs = np.clip(q_r - 4, 0, 56)
    cs = np.clip(q_c - 8, 0, 48)
    row_ok = (k_r[:, None] >= rs[None, :]) & (k_r[:, None] < rs[None, :] + 8)
    col_ok = (k_c[:, None] >= cs[None, :]) & (k_c[:, None] < cs[None, :] + 16)
    dr = np.clip(k_r[:, None] - q_r[None, :] + 7, 0, 14)
    dc = np.clip(k_c[:, None] - q_c[None, :] + 15, 0, 30)
    return row_ok & col_ok, dr, dc


def _b_tiles(n):
    return [t for t in range(NT) if _na_vis(n, t)[0].any()]


EDGE_BLOCKS = (0, 1, 30, 31)


def _bias_id(n, t):
    if n in EDGE_BLOCKS:
        return 5 + 4 * EDGE_BLOCKS.index(n) + (t - _b_tiles(n)[0])
    return t - (n - 2)


def _host_consts(rpb):
    bias = np.full((21, 128, 8, 128), NEG, np.float32)
    done = set()
    for n in (2,) + EDGE_BLOCKS:
        for t in _b_tiles(n):
            vis, dr, dc = _na_vis(n, t)
            g = rpb[:, dr, dc]
            g = np.where(vis[None], g, np.float32(NEG))
            bias[_bias_id(n, t)] = np.transpose(g, (1, 0, 2))
            done.add(_bias_id(n, t))
    assert len(done) == 21
    bias = bias.reshape(21, 128, 1024)
    quarter = 16
    inv_freq = (1.0 / (10000.0 ** (np.arange(quarter, dtype=np.float32) / quarter))).astype(np.float32)
    t = np.arange(S)
    rows = (t // 64).astype(np.float32)
    cols = (t % 64).astype(np.float32)
    ar = rows[:, None] * inv_freq[None, :]
    ac = cols[:, None] * inv_freq[None, :]
    C = np.concatenate([np.cos(ar), np.cos(ar), np.cos(ac), np.cos(ac)], -1).astype(np.float32)
    Sn = np.concatenate([-np.sin(ar), np.sin(ar), -np.sin(ac), np.sin(ac)], -1).astype(np.float32)
    rope = np.stack([C.reshape(NT, 128, 64), Sn.reshape(NT, 128, 64)], 2)
    j = np.arange(128)[:, None]
    i = np.arange(128)[None, :]
    lo = np.where(j >= i, 0.0, NEG).astype(np.float32)
    hi = np.where(j <= i, 0.0, NEG).astype(np.float32)
    maskA = np.stack([np.tile(lo, (1, 4)), np.tile(hi, (1, 4))], 0)
    return bias, np.ascontiguousarray(rope), maskA


def build(debug=False, phases=('s0', 'a', 'b'), nt_a=None):
    nc = bass.Bass("TRN2", target_bir_lowering=False)
    din = lambda n, s: nc.dram_tensor(n, list(s), F32, kind="ExternalInput").ap()
    x_d = din("x", [S, D])
    ctx_d = din("ctx", [CTX, D])
    cT_d = din("cT", [128, 8, 2])
    wmod_d = din("w_mod", [D, 6 * D])
    bmod_d = din("b_mod", [1, 6 * D])
    n1g_d = din("n1g", [1, D])
    n2g_d = din("n2g", [1, D])
    win_d = din("w_in", [D, INC])
    gains_d = din("gains", [1, 4 * HD])
    sink_d = din("sink", [1, 8])
    on_d = din("on", [1, D])
    wout_d = din("w_out", [D, D])
    wg_d = din("w_gate", [D, FH])
    wu_d = din("w_up", [D, FH])
    wd_d = din("w_down", [FH, D])
    ident_d = din("ident", [128, 128])
    rope_d = din("rope", [NT, 128, 2, 64])
    maskA_d = din("maskA", [2, 128, 512])
    biasB_d = din("biasB", [21, 128, 1024])
    out_d = nc.dram_tensor("out", [S, D], F32, kind="ExternalOutput").ap()
    xs_d = nc.dram_tensor("xs_scratch", [S, D], F32, kind="Internal").ap()
    dbg_outs = {}

    outer = contextlib.ExitStack()
    with outer:
        def mk(stack):
            sb = lambda name, shape, dt=F32: stack.enter_context(nc.sbuf_tensor("t_" + name, list(shape), dt))
            ps = lambda name, shape, dt=F32: stack.enter_context(nc.psum_tensor("p_" + name, list(shape), dt))
            return sb, ps
        osb, _ = mk(outer)
        idf = osb("idf", [128, 128])
        idb = osb("idb", [128, 128], BF16)
        r2all = osb("r2all", [128, NT])
        gw2_bc = osb("gw2_bc", [128, D])
        sh2_bc = osb("sh2_bc", [128, D])
        g2_bc = osb("g2_bc", [128, D])
        wsh = osb("wsh", [128, 8, INC], BF16)

        sa = contextlib.ExitStack()
        with sa:
            asb, _ = mk(sa)
            w_in = wsh
            w_out = asb("w_out_bf", [128, 8, D], BF16)
            gw1_bc = asb("gw1_bc", [128, D])
            sh1_bc = asb("sh1_bc", [128, D])
            on_bc = asb("on_bc", [128, D])
            g1_bc = asb("g1_bc", [128, D])
            hc_bf = asb("hc_bf", [128, 2, D], BF16)
            kcTa = asb("kcTa", [128, 2, CTX], BF16)
            kcTb = asb("kcTb", [128, 4, CTX], BF16)
            vcaug = asb("vcaug", [128, 2, 10, 66], BF16)
            maskA = asb("maskA", [128, 2, 512], BF16)
            biasI = asb("biasI", [128, 5, 1024], BF16)
            gain_bc = asb("gain_bc", [128, 4, HD])
            esink = asb("esink", [128, 8])

            s0 = contextlib.ExitStack()
            with s0:
                sb, ps = mk(s0)
                P = Prog(nc, "s0")
                A = P.add
                cTs = sb("cTs", [128, 8, 2])
                sil = sb("sil", [128, 8, 2])
                srep = biasI[:, 3:5, :].rearrange("p w (k m) -> p w k m", k=8)
                NWM = 4
                wm = [sb("wm%d" % i, [128, 8, 512], BF16) for i in range(NWM)]
                bm = [sb("bm%d" % i, [128, 512]) for i in range(2)]
                sc1_t = sb("sc1_t", [128, D])
                csc1_t = sb("csc1_t", [128, D])
                shc_bc = sb("shc_bc", [128, D])
                gwc_bc = sb("gwc_bc", [128, D])
                n1g_bc = sb("n1g_bc", [128, D])
                cx = sb("cx", [128, D])
                cjunk = sb("cjunk", [128, D], BF16)
                cst = sb("cst", [128, 4])
                ch32 = sb("ch32", [128, D])
                pm = [ps("pm%d" % i, [128, 512]) for i in range(2)]
                pmc = [ps("pmc%d" % i, [128, 512]) for i in range(2)]
                Bn = lambda n: Buf(n)
                b_id, b_idb, b_cT, b_sil, b_srep = Bn("id"), Bn("idb"), Bn("cT"), Bn("sil"), Bn("srep")
                b_wm = [Bn("wm%d" % i) for i in range(NWM)]
                b_bm = [Bn("bm0"), Bn("bm1")]
                b_pm = [Bn("pm0"), Bn("pm1")]
                b_pmc = [Bn("pmc0"), Bn("pmc1")]
                b_tgt = {}
                A("sp", lambda e: e.dma_start(out=idf[:], in_=ident_d), writes=[b_id], dma_key="id")
                A("sp", lambda e: e.dma_start(out=cTs[:], in_=cT_d), writes=[b_cT], dma_key="cT")
                A("dve", lambda e: e.tensor_copy(idb[:], idf[:]), reads=[b_id], writes=[b_idb])
                A("act", lambda e: e.activation(sil[:], cTs[:], AF.Silu), reads=[b_cT], writes=[b_sil])
                for w in range(2):
                    A("dve", lambda e, w=w: e.tensor_copy(srep[:, w, :, :], sil[:, :, w:w + 1].to_broadcast([128, 8, 128])),
                      reads=[b_sil], writes=[b_srep])
                b_n1g = Bn("n1g")
                A("sp", lambda e: e.dma_start(out=n1g_bc[:], in_=n1g_d.partition_broadcast(128)), writes=[b_n1g], dma_key="n1g")
                b_gain, b_esink, b_on = Bn("gain"), Bn("esink"), Bn("on")
                A("sp", lambda e: e.dma_start(out=gain_bc[:].rearrange("p a d -> p (a d)"), in_=gains_d.partition_broadcast(128)),
                  writes=[b_gain], dma_key="gain")
                for a in (0, 2):
                    A("dve", lambda e, a=a: e.tensor_scalar(gain_bc[:, a, :], gain_bc[:, a, :], HD ** -0.5, None, ALU.mult),
                      reads=[b_gain], writes=[b_gain])
                A("sp", lambda e: e.dma_start(out=esink[:], in_=sink_d.partition_broadcast(128)), writes=[b_esink], dma_key="sink")
                A("act", lambda e: e.activation(esink[:], esink[:], AF.Exp), reads=[b_esink], writes=[b_esink])
                A("sp", lambda e: e.dma_start(out=on_bc[:], in_=on_d.partition_broadcast(128)), writes=[b_on], dma_key="on")
                targets = [sh1_bc, sc1_t]
                ctargets = [shc_bc, csc1_t]
                for v in range(2):
                    b_tgt[v] = Bn("tgt%d" % v)
                b_ctgt = [Bn("ctgt0"), Bn("ctgt1")]
                def issue_a_weights():
                    for k in range(8):
                        A("pool", lambda e, k=k: e.dma_start(out=w_in[:, k, :], in_=win_d[k * 128:(k + 1) * 128, :]), writes=[Bn("w_in")], dma_key="w_in")

                for ch in range(4):
                    v, half = ch // 2, ch % 2
                    sl = ch % 2
                    cs = slice(ch * 512, (ch + 1) * 512)
                    hs = slice(half * 512, (half + 1) * 512)
                    ws = ch % NWM
                    A("pool", lambda e, ws=ws, cs=cs: e.dma_start(out=wm[ws][:], in_=wmod_d[:, cs].rearrange("(k p) n -> p k n", p=128)),
                      writes=[b_wm[ws]], dma_key="wm%d" % ws)
                    A("sp", lambda e, sl=sl, cs=cs: e.dma_start(out=bm[sl][:], in_=bmod_d[:, cs].partition_broadcast(128)),
                      writes=[b_bm[sl]], dma_key="bm%d" % sl)
                    for k in range(8):
                        A("pe", lambda e, sl=sl, k=k, ch=ch: e.matmul(pm[sl][:], srep[:, 0, k, :], wm[ch % NWM][:, k, :], start=(k == 0), stop=(k == 7)),
                          reads=[b_srep, b_wm[ch % NWM]], writes=[b_pm[sl]])
                    A("dve", lambda e, sl=sl, v=v, hs=hs: e.tensor_tensor(targets[v][:, hs], pm[sl][:], bm[sl][:], ALU.add),
                      reads=[b_pm[sl], b_bm[sl]], writes=[b_tgt[v]])
                    if v < 2:
                        for k in range(8):
                            A("pe", lambda e, sl=sl, k=k, ch=ch: e.matmul(pmc[sl][:], srep[:, 1, k, :], wm[ch % NWM][:, k, :], start=(k == 0), stop=(k == 7)),
                              reads=[b_srep, b_wm[ch % NWM]], writes=[b_pmc[sl]])
                        A("dve", lambda e, sl=sl, v=v, hs=hs: e.tensor_tensor(ctargets[v][:, hs], pmc[sl][:], bm[sl][:], ALU.add),
                          reads=[b_pmc[sl], b_bm[sl]], writes=[b_ctgt[v]])
                issue_a_weights()
                b_gw1, b_gwc = Bn("gw1"), Bn("gwc")
                A("dve", lambda e: e.scalar_tensor_tensor(gw1_bc[:], sc1_t[:], 1.0, n1g_bc[:], ALU.add, ALU.mult),
                  reads=[b_tgt[1], b_n1g], writes=[b_gw1])
                A("dve", lambda e: e.scalar_tensor_tensor(gwc_bc[:], csc1_t[:], 1.0, n1g_bc[:], ALU.add, ALU.mult),
                  reads=[b_ctgt[1], b_n1g], writes=[b_gwc])
                b_cx, b_cjunk, b_cst, b_ch32, b_hc = Bn("cx"), Bn("cjunk"), Bn("cst"), Bn("ch32"), Bn("hc")
                for t in range(2):
                    A("sp", lambda e, t=t: e.dma_start(out=cx[:], in_=ctx_d[t * 128:(t + 1) * 128, :]), writes=[b_cx], dma_key="cx")
                    A("act", lambda e: e.activation(cjunk[:], cx[:], AF.Square, accum_out=cst[:, 0:1]), reads=[b_cx], writes=[b_cjunk, b_cst])
                    A("act", lambda e: e.activation(cst[:, 1:2], cst[:, 0:1], AF.Ln, bias=EPS, scale=1.0 / D), reads=[b_cst], writes=[b_cst])
                    A("act", lambda e: e.activation(cst[:, 2:3], cst[:, 1:2], AF.Exp, scale=-0.5), reads=[b_cst], writes=[b_cst])
                    A("dve", lambda e: e.scalar_tensor_tensor(ch32[:], cx[:], cst[:, 2:3], gwc_bc[:], ALU.mult, ALU.mult),
                      reads=[b_cx, b_cst, b_gwc], writes=[b_ch32])
                    A("dve", lambda e, t=t: e.tensor_tensor(hc_bf[:, t, :], ch32[:], shc_bc[:], ALU.add),
                      reads=[b_ch32, b_ctgt[0]], writes=[b_hc])
                if 's0' in phases:
                    P.emit()

            pa = contextlib.ExitStack()
            with pa:
                sb, ps = mk(pa)
                P = Prog(nc, "a")
                A = P.add
                Bn = lambda n: Buf(n)
                xt = [sb("xt%d" % i, [128, D]) for i in range(2)]
                xres = [sb("xres%d" % i, [128, D]) for i in range(2)]
                ropet = [sb("ropet%d" % i, [128, 2, 64]) for i in range(2)]
                junk = sb("junk", [128, D], BF16)
                h32 = sb("h32", [128, D])
                xn_bf = sb("xn_bf", [128, D], BF16)
                xnT = sb("xnT", [128, 8, 128], BF16)
                qk32 = sb("qk32", [128, 1664])
                T1 = sb("T1", [128, 1664])
                qk_bf = sb("qk_bf", [128, 1664], BF16)
                st = sb("st", [128, 8])
                ss26 = sb("ss26", [128, 3, 26])
                qTa = sb("qTa", [128, RQ, 4, 128], BF16)
                qTb = sb("qTb", [128, RQ, 4, 128], BF16)
                kTa = sb("kTa", [128, RK, 2, 128], BF16)
                kTb = sb("kTb", [128, RK, 4, 128], BF16)
                vaug = sb("vaug", [128, RKV, 10, 66], BF16)
                Pb = [sb("Pb%d" % i, [128, 7, 512], BF16) for i in range(2)]
                o_sb = sb("o_sb", [128, 16, 64])
                den = sb("den", [128, 2, 16])
                gst = sb("gst", [128, 8])
                m_bf = sb("m_bf", [128, D], BF16)
                mT = sb("mT", [128, 8, 128], BF16)
                biasE = sb("biasE", [128, 4, 1024], BF16)
                NPIN = int(os.environ.get("NPIN", "2"))
                NPSS = int(os.environ.get("NPSS", "2"))
                pin = [ps("pin%d" % i, [128, 512]) for i in range(NPIN)]
                ptr = [ps("ptr%d" % i, [128, 1024], BF16) for i in range(2)]
                pss = [ps("pss%d" % i, [128, 512]) for i in range(NPSS)]
                po = [ps("po%d" % i, [128, 512]) for i in range(2)]

                b_w_in, b_w_out, b_maskA, b_biasI = Bn("w_in"), Bn("w_out"), Bn("maskA"), Bn("biasI")
                b_xt = [Bn("xt0"), Bn("xt1")]
                b_xres = [Bn("xres0"), Bn("xres1")]
                b_rope = [Bn("rope0"), Bn("rope1")]
                b_junk, b_h32, b_xn, b_xnT, b_qk32, b_T1, b_qkbf, b_st, b_ss = (Bn(n) for n in
                    "junk h32 xn xnT qk32 T1 qkbf st ss".split())
                b_qTa = [Bn("qTa%d" % i) for i in range(RQ)]
                b_qTb = [Bn("qTb%d" % i) for i in range(RQ)]
                b_kTa = [Bn("kTa%d" % i) for i in range(RK)]
                b_kTb = [Bn("kTb%d" % i) for i in range(RK)]
                b_v = [Bn("v%d" % i) for i in range(RKV)]
                b_kc, b_vc = Bn("kc"), Bn("vc")
                b_Pb = [[Bn("Pb%d_%d" % (i, c)) for c in range(7)] for i in range(2)]
                b_o, b_den, b_gst, b_m, b_mT, b_biasE = (Bn(n) for n in "o den gst m mT biasE".split())
                b_pin = [Bn("pin%d" % i) for i in range(NPIN)]
                b_ptr = [Bn("ptr0"), Bn("ptr1")]
                b_pss = [Bn("pss%d" % i) for i in range(NPSS)]
                b_po = [Bn("po0"), Bn("po1")]
                b_xs = [Bn("xs%d" % i) for i in range(NT)]
                b_g1 = Bn("g1")
                b_late = {2: b_g1, 3: Bn("sh2"), 4: Bn("gw2"), 5: Bn("g2")}
                b_r2 = Bn("r2")

                A("pool", lambda e: e.memset(vaug[:], 1.0), writes=b_v)
                A("pool", lambda e: e.memset(vcaug[:], 1.0), writes=[b_vc])
                A("pool", lambda e: e.memset(kTa[:], 0.0), writes=b_kTa)
                A("pool", lambda e: e.memset(kcTa[:], 0.0), writes=[b_kc])

                cnt = {"pin": 0, "ptr": 0, "pss": 0, "po": 0, "u": 0, "ev": 0}

                def nxt(k):
                    cnt[k] += 1
                    return (cnt[k] - 1) % {"pin": NPIN, "pss": NPSS}.get(k, 2)

                def pick_eng():
                    cnt["ev"] += 1
                    return "act" if cnt["ev"] % 4 == 0 else "dve"

                def evac(out_ap, in_ap, r, w, eng=None):
                    if eng is None:
                        eng = pick_eng()
                    if eng == "act":
                        A("act", lambda e: e.copy(out_ap, in_ap), reads=r, writes=w)
                    else:
                        A("dve", lambda e: e.tensor_copy(out_ap, in_ap), reads=r, writes=w)

                def rstd(acc_ap, tmp_ap, out_ap, n, bufs):
                    A("act", lambda e: e.activation(tmp_ap, acc_ap, AF.Ln, bias=EPS, scale=1.0 / n), reads=bufs, writes=bufs, cost=0.2)
                    A("act", lambda e: e.activation(out_ap, tmp_ap, AF.Exp, scale=-0.5), reads=bufs, writes=bufs, cost=0.2)

                def transposes_and_inproj(src_bf_ap, b_src, chunks):
                    tb = nxt("ptr")
                    for k in range(8):
                        A("pe", lambda e, k=k, tb=tb: e.transpose(ptr[tb][:, k * 128:(k + 1) * 128], src_bf_ap[:, k * 128:(k + 1) * 128], idb[:]),
                          reads=[b_src], writes=[b_ptr[tb]])
                    evac(xnT[:].rearrange("p k t -> p (k t)"), ptr[tb][:], [b_ptr[tb]], [b_xnT])
                    yield
                    for (c0, c1, consumer) in chunks:
                        pb = nxt("pin")
                        for k in range(8):
                            A("pe", lambda e, k=k, pb=pb, c0=c0, c1=c1: e.matmul(pin[pb][:, 0:c1 - c0], xnT[:, k, :], w_in[:, k, c0:c1],
                                                                                start=(k == 0), stop=(k == 7)),
                              reads=[b_xnT, b_w_in], writes=[b_pin[pb]], cost=0.257)
                            if k == 3:
                                yield
                        consumer(pin[pb], b_pin[pb])
                        yield

                def qk_chain(is_ctx, rope_slot):
                    A("act", lambda e: e.activation(T1[:], qk32[:], AF.Square), reads=[b_qk32], writes=[b_T1], cost=1.5)
                    yield
                    A("dve", lambda e: e.tensor_reduce(ss26[:, 0, :], T1[:].rearrange("p (h d) -> p h d", d=HD), AX.X, ALU.add),
                      reads=[b_T1], writes=[b_ss], cost=1.8)
                    yield
                    rstd(ss26[:, 0, :], ss26[:, 1, :], ss26[:, 2, :], HD, [b_ss])
                    yield
                    A("dve", lambda e: e.tensor_tensor(T1[:].rearrange("p (h d) -> p h d", d=HD), qk32[:].rearrange("p (h d) -> p h d", d=HD),
                                                       ss26[:, 2, :].unsqueeze(2).to_broadcast([128, 26, HD]), ALU.mult),
                      reads=[b_qk32, b_ss], writes=[b_T1], cost=1.9)
                    yield

                    def gain_mul(eng, out_ap, c0, nh, gi, wbuf):
                        A(eng, lambda e: e.tensor_tensor(out_ap, T1[:, c0:c0 + nh * HD].rearrange("p (h d) -> p h d", d=HD),
                                                         gain_bc[:, gi:gi + 1, :].to_broadcast([128, nh, HD]), ALU.mult),
                          reads=[b_T1], writes=[wbuf], cost=(0.8 if eng == "dve" else 1.3) * nh / 8 + 0.1)
                    gain_mul("pool", qk_bf[:, 640:1152].rearrange("p (h d) -> p h d", d=HD), 640, 8, 2, b_qkbf)
                    gain_mul("dve", qk_bf[:, 1152:1664].rearrange("p (h d) -> p h d", d=HD), 1152, 8, 3, b_qkbf)
                    yield
                    if is_ctx:
                        gain_mul("pool", qk_bf[:, 512:640].rearrange("p (h d) -> p h d", d=HD), 512, 2, 1, b_qkbf)
                        return
                    gain_mul("pool", qk32[:, 0:512].rearrange("p (h d) -> p h d", d=HD), 0, 8, 0, b_qk32)
                    gain_mul("dve", qk32[:, 512:640].rearrange("p (h d) -> p h d", d=HD), 512, 2, 1, b_qk32)
                    yield
                    rp = ropet[rope_slot]
                    src = qk32[:, 0:640].rearrange("p (h d) -> p h d", d=HD)
                    A("dve", lambda e: e.tensor_tensor(T1[:, 0:640].rearrange("p (h d) -> p h d", d=HD), src,
                                                       rp[:, 0:1, :].to_broadcast([128, 10, HD]), ALU.mult),
                      reads=[b_qk32, b_rope[rope_slot]], writes=[b_T1])
                    yield
                    s5 = qk32[:, 0:640].rearrange("p (h a b d) -> p h a b d", a=2, b=2, d=16)
                    v5 = T1[:, 640:1280].rearrange("p (h a b d) -> p h a b d", a=2, b=2, d=16)
                    r5 = rp[:, 1, :].rearrange("p (a b d) -> p a b d", a=2, b=2, d=16)
                    for blk, eng in ((0, "pool"), (1, "dve")):
                        A(eng, lambda e, blk=blk: e.tensor_tensor(v5[:, :, :, blk, :], s5[:, :, :, 1 - blk, :],
                                                                  r5[:, :, blk, :].unsqueeze(1).to_broadcast([128, 10, 2, 16]), ALU.mult),
                          reads=[b_qk32, b_rope[rope_slot]], writes=[b_T1])
                    yield
                    A("dve", lambda e: e.tensor_tensor(qk_bf[:, 0:512].rearrange("p (j g d) -> p g j d", j=4, g=2, d=HD),
                                                       T1[:, 0:512].rearrange("p (g j d) -> p g j d", j=4, g=2, d=HD),
                                                       T1[:, 640:1152].rearrange("p (g j d) -> p g j d", j=4, g=2, d=HD), ALU.add),
                      reads=[b_T1], writes=[b_qkbf])
                    A("pool", lambda e: e.tensor_tensor(qk_bf[:, 512:640], T1[:, 512:640], T1[:, 1152:1280], ALU.add),
                      reads=[b_T1], writes=[b_qkbf])

                def qk_transposes(dst_qa, dst_ka, dst_qb, dst_kb, wq, wk):
                    tb = nxt("ptr")
                    for j in range(5):
                        if dst_qa is None and j < 4:
                            continue
                        A("pe", lambda e, j=j, tb=tb: e.transpose(ptr[tb][:, j * 128:(j + 1) * 128], qk_bf[:, j * 128:(j + 1) * 128], idb[:]),
                          reads=[b_qkbf], writes=[b_ptr[tb]])
                    en = pick_eng()
                    if dst_qa is not None:
                        evac(dst_qa, ptr[tb][:, 0:512], [b_ptr[tb]], wq, en)
                    for g_ in range(2):
                        evac(dst_ka[g_ * 64:(g_ + 1) * 64, g_, :], ptr[tb][g_ * 64:(g_ + 1) * 64, 512:640], [b_ptr[tb]], wk, en)
                    yield
                    tb = nxt("ptr")
                    for j in range(8):
                        if dst_qb is None and j < 4:
                            continue
                        A("pe", lambda e, j=j, tb=tb: e.transpose(ptr[tb][:, j * 128:(j + 1) * 128], qk_bf[:, 640 + j * 128:640 + (j + 1) * 128], idb[:]),
                          reads=[b_qkbf], writes=[b_ptr[tb]])
                    en = pick_eng()
                    if dst_qb is not None:
                        evac(dst_qb, ptr[tb][:, 0:512], [b_ptr[tb]], wq, en)
                    srckb = ptr[tb][:, 512:1024]
                    if len(dst_kb.shape) == 3:
                        srckb = srckb.rearrange("p (j t) -> p j t", j=4)
                    evac(dst_kb, srckb, [b_ptr[tb]], wk, en)
                    yield

                def inproj_consumers(v_dst, b_vdst, with_q):
                    def c0(pt, bpt):
                        evac(qk32[:, 0:512], pt[:, 0:512], [bpt], [b_qk32])

                    def c1(pt, bpt):
                        en = pick_eng()
                        evac(qk32[:, 512:640], pt[:, 0:128], [bpt], [b_qk32], en)
                        evac(v_dst[:, 0:2, 0:64], pt[:, 128:256].rearrange("p (h d) -> p h d", d=HD), [bpt], [b_vdst], en)
                        if with_q:
                            evac(qk32[:, 640:896], pt[:, 256:512], [bpt], [b_qk32], en)

                    def c2(pt, bpt):
                        en = pick_eng()
                        if with_q:
                            evac(qk32[:, 896:1152], pt[:, 0:256], [bpt], [b_qk32], en)
                        evac(qk32[:, 1152:1408], pt[:, 256:512], [bpt], [b_qk32], en)

                    def c3(pt, bpt):
                        en = pick_eng()
                        evac(qk32[:, 1408:1664], pt[:, 0:256], [bpt], [b_qk32], en)
                        evac(v_dst[:, 2:6, 0:64], pt[:, 256:512].rearrange("p (h d) -> p h d", d=HD), [bpt], [b_vdst], en)

                    def c4(pt, bpt):
                        evac(v_dst[:, 6:10, 0:64], pt[:, 0:256].rearrange("p (h d) -> p h d", d=HD), [bpt], [b_vdst])
                    ch = [(512, 1024, c1), (1024, 1536, c2), (1536, 2048, c3), (2048, 2304, c4)]
                    if with_q:
                        ch = [(0, 512, c0)] + ch
                    return ch

                A("pool", lambda e: e.dma_start(out=maskA[:], in_=maskA_d.rearrange("a p n -> p a n")), writes=[b_maskA], dma_key="maskA")
                for k in range(8):
                    A("pool", lambda e, k=k: e.dma_start(out=w_out[:, k, :], in_=wout_d[k * 128:(k + 1) * 128, :]), writes=[b_w_out], dma_key="w_out")
                A_STOP = os.environ.get("A_STOP", "")
                b_hc2 = Bn("hc2")
                for t in range(2 if A_STOP != "w" else 0):
                    if not True:
                        pass
                    if t == 0:
                        A("dve", lambda e: e.memset(qk32[:], 1.0), writes=[b_qk32])
                    for _ in transposes_and_inproj(hc_bf[:, t, :], b_hc2, inproj_consumers(vcaug[:, t], b_vc, False)):
                        pass
                    for _ in qk_chain(True, 0):
                        pass
                    for _ in qk_transposes(None, kcTa[:, :, t * 128:(t + 1) * 128], None,
                                           kcTb[:, :, t * 128:(t + 1) * 128], None, [b_kc]):
                        pass

                def load_x(i):
                    s = i % 2
                    A("sp", lambda e: e.dma_start(out=xt[s][:], in_=x_d[i * 128:(i + 1) * 128, :]), writes=[b_xt[s]], dma_key="xt%d" % s)

                def load_rope(i):
                    s = i % 2
                    A("sp", lambda e: e.dma_start(out=ropet[s][:], in_=rope_d[i]), writes=[b_rope[s]], dma_key="rope%d" % s)

                def stage_norm(i):
                    s = i % 2
                    A("act", lambda e: e.activation(junk[:], xt[s][:], AF.Square, accum_out=st[:, 0:1]), reads=[b_xt[s]], writes=[b_junk, b_st], cost=1.06)
                    rstd(st[:, 0:1], st[:, 1:2], st[:, 2:3], D, [b_st])
                    yield
                    A("dve", lambda e: e.scalar_tensor_tensor(h32[:], xt[s][:], st[:, 2:3], gw1_bc[:], ALU.mult, ALU.mult),
                      reads=[b_xt[s], b_st], writes=[b_h32], cost=1.2)
                    yield
                    A("dve", lambda e: e.tensor_tensor(xn_bf[:], h32[:], sh1_bc[:], ALU.add), reads=[b_h32], writes=[b_xn], cost=1.26)
                    yield

                def stage_inproj(i):
                    yield from transposes_and_inproj(xn_bf[:], b_xn, inproj_consumers(vaug[:, i % RKV], b_v[i % RKV], True))

                def stage_qkchain(i):
                    yield from qk_chain(False, i % 2)

                def stage_qkT(i):
                    qs, ks = i % RQ, i % RK
                    yield from qk_transposes(qTa[:, qs].rearrange("p j t -> p (j t)"), kTa[:, ks],
                                             qTb[:, qs].rearrange("p j t -> p (j t)"), kTb[:, ks].rearrange("p j t -> p (j t)"),
                                             [b_qTa[qs], b_qTb[qs]], [b_kTa[ks], b_kTb[ks]])

                def ksrc(kind, idx):
                    if kind == "ctx":
                        return (kcTa[:, :, idx * 128:(idx + 1) * 128], kcTb[:, :, idx * 128:(idx + 1) * 128], vcaug[:, idx], [b_kc], [b_vc])
                    s = idx % RK
                    sv = idx % RKV
                    return (kTa[:, s], kTb[:, s], vaug[:, sv], [b_kTa[s], b_kTb[s]], [b_v[sv]])

                def units_for(n):
                    qs = n % RQ
                    a_chunks = []
                    for t, mk_ in ((n - 1, 0), (n, None), (n + 1, 1)):
                        if 0 <= t < NT:
                            a_chunks.append(("tile", t, mk_))
                    a_chunks += [("ctx", 0, None), ("ctx", 1, None)]
                    b_chunks = [("tile", t, _bias_id(n, t)) for t in _b_tiles(n)] + [("ctx", 0, None), ("ctx", 1, None)]
                    us = []
                    for g in range(2):
                        us.append(dict(kind="A", g=g, chunks=a_chunks, qs=qs, n=n))
                    for par in range(2):
                        us.append(dict(kind="B", g=par, chunks=b_chunks, qs=qs, n=n))
                    return us

                def unit_qk(u, pbi):
                    base = u["g"] * 64
                    qs = u["qs"]
                    n = u["n"]
                    for ci, (kind, idx, extra) in enumerate(u["chunks"]):
                        ka, kb, _, kbufs, _ = ksrc(kind, idx)
                        sbk = nxt("pss")
                        pst = pss[sbk]
                        if u["kind"] == "A":
                            if extra is not None:
                                A("pe", lambda e, pst=pst, extra=extra: e.matmul(pst[:], idb[:], maskA[:, extra, :], start=True, stop=False),
                                  reads=[b_maskA], writes=[b_pss[sbk]], cost=0.257)
                            A("pe", lambda e, pst=pst, ka=ka, extra=extra: e.matmul(
                                pst[:], ka[:, u["g"], :], qTa[:, qs].rearrange("p j t -> p (j t)"),
                                start=(extra is None), stop=True),
                              reads=kbufs + [b_qTa[qs]], writes=[b_pss[sbk]], cost=0.26)
                        else:
                            if extra is not None:
                                if n in EDGE_BLOCKS:
                                    e_i = extra - (5 + 4 * EDGE_BLOCKS.index(n))
                                    bt, bbuf = biasE[:, e_i, :], b_biasE
                                else:
                                    bt, bbuf = biasI[:, extra, :], b_biasI
                                bsel = bt.rearrange("p (pr two i) -> p pr two i", pr=4, two=2, i=128)[:, :, u["g"], :]
                                A("pe", lambda e, pst=pst, bsel=bsel: e.matmul(pst[:].rearrange("p (pr i) -> p pr i", pr=4), idb[:], bsel,
                                                                               start=True, stop=False),
                                  reads=[bbuf], writes=[b_pss[sbk]], cost=0.257)
                            for pr in range(4):
                                A("pe", lambda e, pst=pst, kb=kb, pr=pr, extra=extra: e.matmul(
                                    pst[:, pr * 128:(pr + 1) * 128], kb[base:base + 64, pr, :], qTb[base:base + 64, qs, pr, :],
                                    start=(extra is None), stop=(True if extra is None else pr == 3)),
                                  reads=kbufs + [b_qTb[qs]], writes=[b_pss[sbk]], cost=0.083)
                        A("act", lambda e, pst=pst, ci=ci: e.activation(Pb[pbi][:, ci, :], pst[:], AF.Exp), reads=[b_pss[sbk]], writes=[b_Pb[pbi][ci]])
                        yield

                def unit_pv(u, pbi):
                    pob = nxt("po")
                    pot = po[pob][:, 0:260].rearrange("p (h d) -> p h d", d=65)
                    nchunk = len(u["chunks"])
                    for j in range(4):
                        for ci, (kind, idx, extra) in enumerate(u["chunks"]):
                            _, _, v, _, vbufs = ksrc(kind, idx)
                            hv = u["g"] if u["kind"] == "A" else 2 + 2 * j + u["g"]
                            A("pe", lambda e, j=j, ci=ci, v=v, hv=hv: e.matmul(pot[:, j, :], Pb[pbi][:, ci, j * 128:(j + 1) * 128], v[:, hv, 0:65],
                                                                                 start=(ci == 0), stop=(ci == nchunk - 1)),
                              reads=[b_Pb[pbi][ci]] + vbufs, writes=[b_po[pob]], cost=0.056)
                        yield
                    if u["kind"] == "A":
                        heads = [4 * u["g"] + j for j in range(4)]
                        hsl = slice(4 * u["g"], 4 * u["g"] + 4)
                        A("dve", lambda e: e.tensor_tensor(den[:, 0, hsl], pot[:, :, 64], esink[:, hsl], ALU.add), reads=[b_po[pob]], writes=[b_den], cost=0.1)
                        A("dve", lambda e: e.reciprocal(den[:, 1, hsl], den[:, 0, hsl]), reads=[b_den], writes=[b_den], cost=0.18)
                        A("dve", lambda e: e.tensor_tensor(o_sb[:, hsl, :], pot[:, :, 0:64], den[:, 1, hsl].unsqueeze(2).to_broadcast([128, 4, 64]), ALU.mult),
                          reads=[b_po[pob], b_den], writes=[b_o])
                    else:
                        par = u["g"]
                        dv = den[:, :, 8:16].rearrange("p a (pr two) -> p a pr two", two=2)
                        A("dve", lambda e: e.reciprocal(dv[:, 1, :, par], pot[:, :, 64]), reads=[b_po[pob]], writes=[b_den], cost=0.18)
                        ov = o_sb[:, 8:16, :].rearrange("p (pr two) d -> p pr two d", two=2)
                        A("dve", lambda e: e.tensor_tensor(ov[:, :, par, :], pot[:, :, 0:64], dv[:, 1, :, par].unsqueeze(2).to_broadcast([128, 4, 64]), ALU.mult),
                          reads=[b_po[pob], b_den], writes=[b_o])
                    yield

                def merge_prep(n):
                    for g in range(2):
                        osl = o_sb[:, 8 * g:8 * g + 8, :].rearrange("p h d -> p (h d)")
                        A("act", lambda e, g=g, osl=osl: e.activation(junk[:, 0:512], osl, AF.Square, accum_out=gst[:, g:g + 1]),
                          reads=[b_o], writes=[b_junk, b_gst])
                    yield
                    rstd(gst[:, 0:2], gst[:, 2:4], gst[:, 4:6], 512, [b_gst])
                    yield
                    for g, eng in ((0, "dve"), (1, "dve")):
                        osl = o_sb[:, 8 * g:8 * g + 8, :].rearrange("p h d -> p (h d)")
                        A(eng, lambda e, g=g, osl=osl: e.scalar_tensor_tensor(m_bf[:, g * 512:(g + 1) * 512], osl, gst[:, 4 + g:5 + g],
                                                                              on_bc[:, g * 512:(g + 1) * 512], ALU.mult, ALU.mult),
                          reads=[b_o, b_gst], writes=[b_m], cost=0.75)
                        yield

                def out_proj(n):
                    s = n % 2
                    tb = nxt("ptr")
                    for k in range(8):
                        A("pe", lambda e, k=k, tb=tb: e.transpose(ptr[tb][:, k * 128:(k + 1) * 128], m_bf[:, k * 128:(k + 1) * 128], idb[:]),
                          reads=[b_m], writes=[b_ptr[tb]])
                    evac(mT[:].rearrange("p k t -> p (k t)"), ptr[tb][:], [b_ptr[tb]], [b_mT])
                    yield
                    for nch in range(2):
                        pb = nxt("pin")
                        for k in range(8):
                            A("pe", lambda e, k=k, pb=pb, nch=nch: e.matmul(pin[pb][:], mT[:, k, :], w_out[:, k, nch * 512:(nch + 1) * 512],
                                                                             start=(k == 0), stop=(k == 7)),
                              reads=[b_mT, b_w_out], writes=[b_pin[pb]], cost=0.257)
                            if k == 3:
                                yield
                        yield
                        cs = slice(nch * 512, (nch + 1) * 512)
                        A("dve", lambda e, pb=pb, cs=cs: e.tensor_tensor(h32[:, cs], pin[pb][:], g1_bc[:, cs], ALU.mult),
                          reads=[b_pin[pb], b_g1], writes=[b_h32], cost=0.7)
                        A("pool", lambda e, cs=cs: e.tensor_tensor(xres[s][:, cs], xres[s][:, cs], h32[:, cs], ALU.add),
                          reads=[b_h32, b_xres[s]], writes=[b_xres[s]], cost=1.27)
                        yield
                    A("act", lambda e: e.activation(junk[:], xres[s][:], AF.Square, accum_out=gst[:, 6:7]), reads=[b_xres[s]], writes=[b_junk, b_gst], cost=0.95)
                    A("act", lambda e: e.activation(gst[:, 7:8], gst[:, 6:7], AF.Ln, bias=EPS, scale=1.0 / D), reads=[b_gst], writes=[b_gst], cost=0.2)
                    A("act", lambda e: e.activation(r2all[:, n:n + 1], gst[:, 7:8], AF.Exp, scale=-0.5), reads=[b_gst], writes=[b_r2], cost=0.2)
                    A("sp", lambda e: e.dma_start(out=xs_d[n * 128:(n + 1) * 128, :], in_=xres[s][:]), reads=[b_xres[s]], writes=[b_xs[n]],
                      dma_key="xs%d" % s)
                    yield

                pending = []

                def attn_gen(n):
                    for u in units_for(n):
                        pbi = nxt("u")
                        yield from unit_qk(u, pbi)
                        if pending:
                            yield from unit_pv(*pending.pop(0))
                        pending.append((u, pbi))
                    yield from unit_pv(*pending.pop(0))

                def chain_gens(gens):
                    for g in gens:
                        yield from g

                def run_all(g):
                    for _ in g:
                        pass

                def emit_late_mod():
                    srepA = biasI[:, 3, :].rearrange("p (k m) -> p k m", k=8)
                    late_tgt = {2: g1_bc, 3: sh2_bc, 4: gw2_bc, 5: g2_bc}
                    o_flat = o_sb[:].rearrange("p h d -> p (h d)")
                    A("sp", lambda e: e.dma_start(out=xres[0][:], in_=n2g_d.partition_broadcast(128)), writes=[b_xres[0]], dma_key="n2g")
                    for cidx in range(16):
                        v, q_ = 2 + cidx // 4, cidx % 4
                        col0 = v * D + q_ * 256
                        i_ = cidx % 2
                        stg = Pb[i_][:, 0:4, :].rearrange("p a (b c) -> p (a b) c", b=2, c=256)
                        bmst = o_flat[:, i_ * 256:(i_ + 1) * 256]
                        A("pool", lambda e, stg=stg, col0=col0: e.dma_start(out=stg, in_=wmod_d[:, col0:col0 + 256].rearrange("(k p) n -> p k n", p=128)),
                          writes=b_Pb[i_][0:4], dma_key="lm%d" % i_)
                        A("sp", lambda e, bmst=bmst, col0=col0: e.dma_start(out=bmst, in_=bmod_d[:, col0:col0 + 256].partition_broadcast(128)),
                          writes=[b_o], dma_key="lb%d" % i_)
                        pb = nxt("pin")
                        for k in range(8):
                            A("pe", lambda e, k=k, pb=pb, stg=stg: e.matmul(pin[pb][:, 0:256], srepA[:, k, :], stg[:, k, :], start=(k == 0), stop=(k == 7)),
                              reads=[b_biasI] + b_Pb[i_][0:4], writes=[b_pin[pb]], cost=0.12)
                        tg = late_tgt[v][:, q_ * 256:(q_ + 1) * 256]
                        A("dve", lambda e, tg=tg, pb=pb, bmst=bmst: e.tensor_tensor(tg, pin[pb][:, 0:256], bmst, ALU.add),
                          reads=[b_pin[pb], b_o], writes=[b_late[v]])
                    A("dve", lambda e: e.scalar_tensor_tensor(gw2_bc[:], gw2_bc[:], 1.0, xres[0][:], ALU.add, ALU.mult),
                      reads=[b_late[4], b_xres[0]], writes=[b_late[4]])

                LAG = 4
                load_x(0)
                load_x(1)
                load_rope(0)
                for it in range(-1, NT + LAG + 1):
                    n = it - LAG
                    if it == NT:
                        for c0_, c1_ in ((0, 512), (512, 1024), (1024, 1536), (1536, 2048), (2048, INC)):
                            A("pool", lambda e, c0_=c0_, c1_=c1_: e.dma_start(out=wsh[:, :, c0_:c1_], in_=wg_d[:, c0_:c1_].rearrange("(k p) n -> p k n", p=128)),
                              writes=[b_w_in], dma_key="wgpre")
                    if it == 1:
                        emit_late_mod()
                        A("pool", lambda e: e.dma_start(out=biasI[:], in_=biasB_d[0:5].rearrange("a p n -> p a n")), writes=[b_biasI], dma_key="biasI")
                    if 0 <= it + 2 < NT and it >= 0:
                        load_x(it + 2)
                    if 0 <= it + 1 < NT:
                        load_rope(it + 1)
                    if 0 <= n < NT:
                        s = n % 2
                        A("sp", lambda e, n=n, s=s: e.dma_start(out=xres[s][:], in_=x_d[n * 128:(n + 1) * 128, :]), writes=[b_xres[s]],
                          dma_key="xres%d" % s)
                        if n in EDGE_BLOCKS:
                            e0 = 5 + 4 * EDGE_BLOCKS.index(n)
                            A("pool", lambda e, e0=e0: e.dma_start(out=biasE[:], in_=biasB_d[e0:e0 + 4].rearrange("a p n -> p a n")),
                              writes=[b_biasE], dma_key="biasE")
                    pre = []
                    if 0 <= n - 1 < NT:
                        pre.append(merge_prep(n - 1))
                    if 0 <= it < NT:
                        pre.append(stage_inproj(it))
                    if 1 <= it <= NT:
                        pre.append(stage_qkT(it - 1))
                    rest = []
                    if 0 <= n - 1 < NT:
                        rest.append(out_proj(n - 1))
                    if 0 <= it < NT:
                        rest.append(stage_qkchain(it))
                    if 0 <= it + 1 < NT:
                        rest.append(stage_norm(it + 1))
                    if not (0 <= n < NT):
                        run_all(chain_gens(pre + rest))
                        continue
                    if n == 0:
                        run_all(chain_gens(pre))
                        fill = chain_gens(rest)
                    else:
                        fill = chain_gens(pre + rest)
                    for _ in attn_gen(n):
                        next(fill, None)
                    run_all(fill)
                if 'a' in phases:
                    P.emit(window=int(os.environ.get("A_WINDOW", "64")))
                    print("phase A estimate us:", P.est_time)

        sbk = contextlib.ExitStack()
        with sbk:
            sb, ps = mk(sbk)
            P = Prog(nc, "b")
            A = P.add
            Bn = lambda n: Buf(n)
            wgB = sb("wgB", [128, 8, FH - INC], BF16)
            wu = sb("wu", [128, 8, FH], BF16)
            wd = sb("wd", [128, NJ, D], BF16)
            xin = [sb("xin%d" % i, [128, D]) for i in range(2)]
            xrs = [sb("xrs%d" % i, [128, D]) for i in range(2)]
            tmp = sb("tmpb", [128, 512])
            xn2 = sb("xn2", [128, D], BF16)
            xn2T = [sb("xn2T%d" % i, [128, 8, 512], BF16) for i in range(2)]
            actT = sb("actT", [128, NJ, 512], BF16)
            sg = sb("sg", [128, 512])
            pg = [ps("pg%d" % i, [128, 512]) for i in range(2)]
            pu = [ps("pu%d" % i, [128, 512]) for i in range(2)]
            pd = [ps("pd%d" % i, [128, 512]) for i in range(2)]
            ptr = [ps("ptrb%d" % i, [128, 1024], BF16) for i in range(2)]
            NG = (NJ + 3) // 4
            b_wg = [Bn("wg%d" % g) for g in range(NG)]
            b_wu = [Bn("wu%d" % g) for g in range(NG)]
            b_wd = [Bn("wd%d" % g) for g in range(NG)]
            b_xin = [Bn("xin0"), Bn("xin1")]
            b_xrs = [Bn("xrs0"), Bn("xrs1")]
            b_tmp, b_xn2, b_sg = Bn("tmp"), Bn("xn2"), Bn("sg")
            b_xn2T = [Bn("xn2T0"), Bn("xn2T1")]
            b_actT = [Bn("actT%d" % j) for j in range(NJ)]
            b_pg = [Bn("pg0"), Bn("pg1")]
            b_pu = [Bn("pu0"), Bn("pu1")]
            b_pd = [Bn("pd0"), Bn("pd1")]
            b_ptr = [Bn("ptr0"), Bn("ptr1")]
            b_out = [Bn("out%d" % i) for i in range(NT)]
            cnt = {"ptr": 0, "g": 0, "pd": 0}

            def nxt(k):
                cnt[k] += 1
                return (cnt[k] - 1) % 2

            def fold_wd(g):
                j0, j1 = g * 4, min(g * 4 + 4, NJ)
                A("dve", lambda e: e.tensor_tensor(wd[:, j0:j1, :], wd[:, j0:j1, :],
                                                   g2_bc[:].unsqueeze(1).to_broadcast([128, j1 - j0, D]), ALU.mult),
                  reads=[b_wd[g]], writes=[b_wd[g]])

            for g in range(NG):
                c0, c1 = g * 512, min((g + 1) * 512, FH)
                if g == NG - 1:
                    A("pool", lambda e: e.dma_start(out=wgB[:], in_=wg_d[:, INC:FH].rearrange("(k p) n -> p k n", p=128)),
                      writes=[b_wg[g]], dma_key="wg%d" % g)
                A("pool", lambda e, c0=c0, c1=c1: e.dma_start(out=wu[:, :, c0:c1], in_=wu_d[:, c0:c1].rearrange("(k p) n -> p k n", p=128)),
                  writes=[b_wu[g]], dma_key="wu%d" % g)
                j0, j1 = g * 4, min(g * 4 + 4, NJ)
                A("pool", lambda e, j0=j0, j1=j1: e.dma_start(out=wd[:, j0:j1, :], in_=wd_d[j0 * 128:j1 * 128, :].rearrange("(j p) n -> p j n", p=128)),
                  writes=[b_wd[g]], dma_key="wd%d" % g)

            def prep_gen(blk):
                bs = blk % 2
                for tt in range(4):
                    t = blk * 4 + tt
                    s = t % 2
                    A("sp", lambda e, t=t, s=s: e.dma_start(out=xin[s][:], in_=xs_d[t * 128:(t + 1) * 128, :]), writes=[b_xin[s]], dma_key="xin%d" % s)
                    for half in range(2):
                        cs = slice(half * 512, (half + 1) * 512)
                        A("dve", lambda e, t=t, s=s, cs=cs: e.scalar_tensor_tensor(tmp[:], xin[s][:, cs], r2all[:, t:t + 1], gw2_bc[:, cs], ALU.mult, ALU.mult),
                          reads=[b_xin[s]], writes=[b_tmp])
                        yield
                        A("dve", lambda e, cs=cs: e.tensor_tensor(xn2[:, cs], tmp[:], sh2_bc[:, cs], ALU.add), reads=[b_tmp], writes=[b_xn2])
                        yield
                    tb = nxt("ptr")
                    for k in range(8):
                        A("pe", lambda e, k=k, tb=tb: e.transpose(ptr[tb][:, k * 128:(k + 1) * 128], xn2[:, k * 128:(k + 1) * 128], idb[:]),
                          reads=[b_xn2], writes=[b_ptr[tb]])
                    yield
                    A("act", lambda e, tb=tb, tt=tt, bs=bs: e.copy(xn2T[bs][:, :, tt * 128:(tt + 1) * 128], ptr[tb][:].rearrange("p (k t) -> p k t", k=8)),
                      reads=[b_ptr[tb]], writes=[b_xn2T[bs]])
                    yield

            def gateup_gen(blk):
                bs = blk % 2
                for j in range(NJ):
                    gb = nxt("g")
                    gi = j // 4
                    for k in range(8):
                        wsl = wsh[:, k, j * 128:(j + 1) * 128] if j < INC // 128 else wgB[:, k, (j - INC // 128) * 128:(j - INC // 128 + 1) * 128]
                        A("pe", lambda e, k=k, gb=gb, wsl=wsl: e.matmul(pg[gb][:], wsl, xn2T[bs][:, k, :], start=(k == 0), stop=(k == 7)),
                          reads=[b_wg[NG - 1 if j >= INC // 128 else gi], b_xn2T[bs]], writes=[b_pg[gb]])
                    for k in range(8):
                        A("pe", lambda e, k=k, j=j, gb=gb: e.matmul(pu[gb][:], wu[:, k, j * 128:(j + 1) * 128], xn2T[bs][:, k, :], start=(k == 0), stop=(k == 7)),
                          reads=[b_wu[gi], b_xn2T[bs]], writes=[b_pu[gb]])
                    A("act", lambda e, gb=gb: e.activation(sg[:], pg[gb][:], AF.Silu), reads=[b_pg[gb]], writes=[b_sg])
                    A("dve", lambda e, gb=gb, j=j: e.tensor_tensor(actT[:, j, :], pu[gb][:], sg[:], ALU.mult),
                      reads=[b_pu[gb], b_sg], writes=[b_actT[j]])
                    if blk == 0:
                        for g in range(NG):
                            if j == min(4 * g + 11, NJ - 1) or (j == NJ - 1 and 4 * g + 11 > NJ - 1):
                                fold_wd(g)
                    yield

            def down(blk):
                for tt in range(4):
                    t = blk * 4 + tt
                    s = t % 2
                    A("sp", lambda e, t=t, s=s: e.dma_start(out=xrs[s][:], in_=xs_d[t * 128:(t + 1) * 128, :]), writes=[b_xrs[s]], dma_key="xrs%d" % s)
                    for nch in range(2):
                        db = nxt("pd")
                        cs = slice(nch * 512, (nch + 1) * 512)
                        for j in range(NJ):
                            A("pe", lambda e, j=j, db=db, tt=tt, cs=cs: e.matmul(pd[db][:], actT[:, j, tt * 128:(tt + 1) * 128], wd[:, j, cs],
                                                                                  start=(j == 0), stop=(j == NJ - 1)),
                              reads=[b_actT[j], b_wd[j // 4]], writes=[b_pd[db]])
                        A("dve", lambda e, db=db, cs=cs, s=s: e.tensor_tensor(xrs[s][:, cs], pd[db][:], xrs[s][:, cs], ALU.add),
                          reads=[b_pd[db], b_xrs[s]], writes=[b_xrs[s]])
                    A("sp", lambda e, t=t, s=s: e.dma_start(out=out_d[t * 128:(t + 1) * 128, :], in_=xrs[s][:]), reads=[b_xrs[s]], writes=[b_out[t]],
                      dma_key="o%d" % s)

            for _ in prep_gen(0):
                pass
            for blk in range(NT // 4):
                fill = prep_gen(blk + 1) if blk + 1 < NT // 4 else iter(())
                for _ in gateup_gen(blk):
                    next(fill, None)
                for _ in fill:
                    pass
                down(blk)
            if 'b' in phases:
                P.emit()
    return nc, dbg_outs


_NC_CACHE = {}


def _prep_inputs(inputs, b):
    f = lambda a: np.ascontiguousarray(np.asarray(a, dtype=np.float32))
    c = f(inputs["c"])[b]
    c_ctx = f(inputs["c_ctx"])
    cT = np.stack([c.reshape(8, 128).T, c_ctx.reshape(8, 128).T], -1)
    gains = np.concatenate([f(inputs["qn_a"])[0], f(inputs["kn_a"])[0], f(inputs["qn_b"])[0], f(inputs["kn_b"])[0]])[None, :]
    on = np.concatenate([f(inputs["on_a"])[0], f(inputs["on_b"])[0]])[None, :]
    return {
        "x": f(inputs["x"])[b], "ctx": f(inputs["ctx"])[b], "cT": np.ascontiguousarray(cT),
        "w_mod": f(inputs["w_mod"])[0], "b_mod": f(inputs["b_mod"])[0][None, :],
        "n1g": f(inputs["norm1_g"])[0][None, :], "n2g": f(inputs["norm2_g"])[0][None, :],
        "w_in": f(inputs["w_in"])[0], "gains": np.ascontiguousarray(gains), "sink": f(inputs["sink_a"])[0][None, :],
        "on": np.ascontiguousarray(on), "w_out": f(inputs["w_out"])[0],
        "w_gate": f(inputs["w_gate"])[0], "w_up": f(inputs["w_up"])[0], "w_down": f(inputs["w_down"])[0],
    }


def kernel(**inputs):
    if "nc" not in _NC_CACHE:
        _NC_CACHE["nc"] = build(False)[0]
    nc = _NC_CACHE["nc"]
    bias, rope, maskA = _host_consts(np.asarray(inputs["rpb_b"], dtype=np.float32)[0])
    consts = {"ident": np.eye(128, dtype=np.float32), "rope": rope, "maskA": maskA, "biasB": bias}
    in_maps = []
    for b in range(8):
        m = _prep_inputs(inputs, b)
        m.update(consts)
        in_maps.append(m)
    res = run_bass_kernel_spmd(nc, in_maps, core_ids=list(range(8)))
    return np.stack([np.asarray(r["out"], dtype=np.float32) for r in res.results], 0)
```
